# Optimizing a Trainium2 kernel written in Bass

```python
import jax, jax.numpy as jnp
from jax import lax
import numpy as np

D_MODEL = 1024
BATCH = 16
SEQ = 2048
DEPTH = 1
DEC_BATCH = 16
DEC_SEQ = 32
PAST_LEN = 1024

CHUNK = 64
N_META = 16
Q_BLOCK = 128
EPS = 1e-6
GLA_HEADS = 4
GLA_DK = 64
GLA_DV = 128
GLA_RANK = 16
GLA_TAU = 16.0
GLA_QK = GLA_HEADS * GLA_DK
GLA_V = GLA_HEADS * GLA_DV
FOX_HEADS = 8
FOX_DH = 64
FOX_W = FOX_HEADS * FOX_DH
FORGET_BIAS_INIT = 4.0
D_FF = 4 * D_MODEL
SPLIT_SIZES = (GLA_QK, GLA_QK, GLA_V, GLA_V, GLA_RANK, FOX_W, FOX_W, FOX_W, FOX_HEADS, D_MODEL, D_MODEL)
D_IN = GLA_QK + GLA_QK + GLA_V + GLA_V + GLA_RANK + FOX_W + FOX_W + FOX_W + FOX_HEADS + D_MODEL + D_MODEL
NEG = -1e30

kernel_name = 'hybrid_gla_fox_stream_step'


def _rmsnorm(x, g):
    xf = x.astype(jnp.float32)
    y = xf * lax.rsqrt(jnp.mean(xf * xf, axis=-1, keepdims=True) + EPS)
    return (y * g.astype(jnp.float32)).astype(x.dtype)


def _in_proj(hn, w_in):
    z = jnp.einsum('btd,de->bte', hn, w_in)
    cuts = [int(c) for c in np.cumsum(SPLIT_SIZES)[:-1]]
    return jnp.split(z, cuts, axis=-1)


def _gla_heads(gq, gk, gv, glr, w_gate, b_gate):
    B, T = gq.shape[0], gq.shape[1]
    f32 = jnp.float32
    q = gq.reshape(B, T, GLA_HEADS, GLA_DK).astype(f32) * (GLA_DK ** -0.5)
    k = gk.reshape(B, T, GLA_HEADS, GLA_DK).astype(f32)
    v = gv.reshape(B, T, GLA_HEADS, GLA_DV).astype(f32)
    z = jnp.einsum('btr,rk->btk', glr, w_gate) + b_gate
    log_a = (jax.nn.log_sigmoid(z.astype(f32)) / GLA_TAU).reshape(B, T, GLA_HEADS, GLA_DK)
    return q, k, v, log_a


def _gla_chunks(q, k, v, log_a, s0):
    g = jnp.cumsum(log_a, axis=2)
    g_tot = g[:, :, -1]
    k_dec = k * jnp.exp(g_tot[:, :, None] - g)
    u = jnp.einsum('bnchk,bnchv->bnhkv', k_dec, v)

    def step(s, inp):
        a, uc = inp
        s = jnp.exp(a)[..., None] * s + uc
        return s, s

    s_last, s_all = lax.scan(step, s0, (jnp.moveaxis(g_tot, 1, 0), jnp.moveaxis(u, 1, 0)))
    s_all = jnp.moveaxis(s_all, 0, 1)
    o = jnp.einsum('bnchk,bnhkv->bnchv', q, s_all)
    return o, s_last


def _gla_out(o, gr, g_norm, w_branch, dtype):
    B, T = o.shape[0], o.shape[1]
    on = _rmsnorm(o, g_norm)
    gate = jax.nn.silu(gr.astype(jnp.float32)).reshape(B, T, GLA_HEADS, GLA_DV)
    y = (on * gate).reshape(B, T, GLA_V).astype(dtype)
    return jnp.einsum('btv,vd->btd', y, w_branch)


def _fox_heads(fq, fk, fv, ff, b_f):
    B, T = fq.shape[0], fq.shape[1]
    sh = lambda a: a.reshape(B, T, FOX_HEADS, FOX_DH).transpose(0, 2, 1, 3)
    logf = jax.nn.log_sigmoid(ff.astype(jnp.float32) + b_f.astype(jnp.float32)).transpose(0, 2, 1)
    return sh(fq), sh(fk), sh(fv), logf


def _fox_attend(q, k, v, cq, ck, qpos, kpos):
    f32 = jnp.float32
    s = jnp.einsum('bhqd,bhkd->bhqk', q.astype(f32), k.astype(f32)) * (FOX_DH ** -0.5)
    s = s + cq[..., :, None] - ck[..., None, :]
    s = jnp.where(kpos[None, :] <= qpos[:, None], s, NEG)
    p = jax.nn.softmax(s, axis=-1)
    return jnp.einsum('bhqk,bhkd->bhqd', p, v.astype(f32))


def _fox_out(o, w_branch, dtype):
    B, T = o.shape[0], o.shape[2]
    y = o.transpose(0, 2, 1, 3).reshape(B, T, FOX_W).astype(dtype)
    return jnp.einsum('btf,fd->btd', y, w_branch)


def _merge_ffn(h, ya, yb, ga, gb, w_out, norm_ffn, w_up, w_down):
    m = jax.nn.sigmoid(ga) * ya + jax.nn.sigmoid(gb) * yb
    h = h + jnp.einsum('btd,de->bte', m, w_out)
    u = jnp.einsum('btd,df->btf', _rmsnorm(h, norm_ffn), w_up)
    return h + jnp.einsum('btf,fd->btd', jnp.square(jax.nn.relu(u)), w_down)


def _layer_prompt(h, norm_mix, w_in, w_gla_gate, b_gla_gate, g_gla_norm, b_fox_forget,
                  w_branch_gla, w_branch_fox, w_out, norm_ffn, w_up, w_down):
    B, L = h.shape[0], h.shape[1]
    n_real = L - N_META
    nc = n_real // CHUNK
    nb = n_real // Q_BLOCK
    hn = _rmsnorm(h, norm_mix)
    gq, gk, gv, gr, glr, fq, fk, fv, ff, ga, gb = _in_proj(hn, w_in)
    q, k, v, log_a = _gla_heads(gq, gk, gv, glr, w_gla_gate, b_gla_gate)
    lead = lambda a: a[:, None, :N_META]
    body = lambda a: a[:, N_META:].reshape((B, nc, CHUNK) + a.shape[2:])
    s0 = jnp.zeros((B, GLA_HEADS, GLA_DK, GLA_DV), jnp.float32)
    o_meta, s_meta = _gla_chunks(lead(q), lead(k), lead(v), lead(log_a), s0)
    o_body, s_last = _gla_chunks(body(q), body(k), body(v), body(log_a), s_meta)
    o = jnp.concatenate([o_meta.reshape(B, N_META, GLA_HEADS, GLA_DV),
                         o_body.reshape(B, n_real, GLA_HEADS, GLA_DV)], axis=1)
    ya = _gla_out(o, gr, g_gla_norm, w_branch_gla, h.dtype)
    fq_h, fk_h, fv_h, logf = _fox_heads(fq, fk, fv, ff, b_fox_forget)
    c = jnp.cumsum(logf, axis=-1)
    pos = jnp.arange(L)
    out_meta = _fox_attend(fq_h[:, :, :N_META], fk_h[:, :, :N_META], fv_h[:, :, :N_META],
                           c[..., :N_META], c[..., :N_META], pos[:N_META], pos[:N_META])
    qb = jnp.moveaxis(fq_h[:, :, N_META:].reshape(B, FOX_HEADS, nb, Q_BLOCK, FOX_DH), 2, 0)
    cb = jnp.moveaxis(c[..., N_META:].reshape(B, FOX_HEADS, nb, Q_BLOCK), 2, 0)
    pb = pos[N_META:].reshape(nb, Q_BLOCK)
    out_body = lax.map(lambda a: _fox_attend(a[0], fk_h, fv_h, a[1], c, a[2], pos), (qb, cb, pb))
    out_body = jnp.moveaxis(out_body, 0, 2).reshape(B, FOX_HEADS, n_real, FOX_DH)
    yb = _fox_out(jnp.concatenate([out_meta, out_body], axis=2), w_branch_fox, h.dtype)
    h = _merge_ffn(h, ya, yb, ga, gb, w_out, norm_ffn, w_up, w_down)
    return h, (fk_h, fv_h, logf, s_last)


def _layer_sample(h, cache_k, cache_v, cache_logf, state, norm_mix, w_in, w_gla_gate, b_gla_gate,
                  g_gla_norm, b_fox_forget, w_branch_gla, w_branch_fox, w_out, norm_ffn, w_up, w_down):
    B, T = h.shape[0], h.shape[1]
    past = cache_k.shape[2]
    hn = _rmsnorm(h, norm_mix)
    gq, gk, gv, gr, glr, fq, fk, fv, ff, ga, gb = _in_proj(hn, w_in)
    q, k, v, log_a = _gla_heads(gq, gk, gv, glr, w_gla_gate, b_gla_gate)
    o, s_new = _gla_chunks(q[:, None], k[:, None], v[:, None], log_a[:, None], state.astype(jnp.float32))
    ya = _gla_out(o.reshape(B, T, GLA_HEADS, GLA_DV), gr, g_gla_norm, w_branch_gla, h.dtype)
    fq_h, fk_h, fv_h, logf = _fox_heads(fq, fk, fv, ff, b_fox_forget)
    k_all = jnp.concatenate([cache_k.astype(fk_h.dtype), fk_h], axis=2)
    v_all = jnp.concatenate([cache_v.astype(fv_h.dtype), fv_h], axis=2)
    c_all = jnp.cumsum(jnp.concatenate([cache_logf.astype(jnp.float32), logf], axis=-1), axis=-1)
    kpos = jnp.arange(past + T)
    fo = _fox_attend(fq_h, k_all, v_all, c_all[..., past:], c_all, kpos[past:], kpos)
    yb = _fox_out(fo, w_branch_fox, h.dtype)
    h = _merge_ffn(h, ya, yb, ga, gb, w_out, norm_ffn, w_up, w_down)
    return h, (fk_h, fv_h, logf, s_new)


def setup_inputs(seed: int = 0) -> dict:
    key = jax.random.key(seed)
    ks = jax.random.split(key, 24)
    f32 = jnp.float32
    nrm = lambda k, shape, scale: jax.random.normal(k, shape, f32) * scale
    return {
        'x_prompt': nrm(ks[0], (BATCH, SEQ, D_MODEL), 1.0),
        'x_sample': nrm(ks[1], (DEC_BATCH, DEC_SEQ, D_MODEL), 1.0),
        'cache_fox_k': nrm(ks[2], (DEPTH, DEC_BATCH, FOX_HEADS, PAST_LEN, FOX_DH), 1.0),
        'cache_fox_v': nrm(ks[3], (DEPTH, DEC_BATCH, FOX_HEADS, PAST_LEN, FOX_DH), 1.0),
        'cache_fox_logf': jax.nn.log_sigmoid(FORGET_BIAS_INIT + nrm(ks[4], (DEPTH, DEC_BATCH, FOX_HEADS, PAST_LEN), 1.0)),
        'state_gla': nrm(ks[5], (DEPTH, DEC_BATCH, GLA_HEADS, GLA_DK, GLA_DV), 1.0),
        'meta_tokens': nrm(ks[6], (N_META, D_MODEL), 1.0),
        'norm_mix': 1.0 + nrm(ks[7], (DEPTH, D_MODEL), 0.05),
        'w_in': nrm(ks[8], (DEPTH, D_MODEL, D_IN), D_MODEL ** -0.5),
        'w_gla_gate': nrm(ks[9], (DEPTH, GLA_RANK, GLA_QK), GLA_RANK ** -0.5),
        'b_gla_gate': nrm(ks[10], (DEPTH, GLA_QK), 0.01),
        'g_gla_norm': 1.0 + nrm(ks[11], (DEPTH, GLA_DV), 0.05),
        'b_fox_forget': FORGET_BIAS_INIT + nrm(ks[12], (DEPTH, FOX_HEADS), 0.1),
        'w_branch_gla': nrm(ks[13], (DEPTH, GLA_V, D_MODEL), GLA_V ** -0.5),
        'w_branch_fox': nrm(ks[14], (DEPTH, FOX_W, D_MODEL), FOX_W ** -0.5),
        'w_out': nrm(ks[15], (DEPTH, D_MODEL, D_MODEL), D_MODEL ** -0.5),
        'norm_ffn': 1.0 + nrm(ks[16], (DEPTH, D_MODEL), 0.05),
        'w_up': nrm(ks[17], (DEPTH, D_MODEL, D_FF), D_MODEL ** -0.5),
        'w_down': nrm(ks[18], (DEPTH, D_FF, D_MODEL), D_FF ** -0.5),
        'norm_final': 1.0 + nrm(ks[19], (D_MODEL,), 0.05),
    }


def reference(x_prompt, x_sample, cache_fox_k, cache_fox_v, cache_fox_logf, state_gla,
              meta_tokens, norm_mix, w_in, w_gla_gate, b_gla_gate, g_gla_norm, b_fox_forget,
              w_branch_gla, w_branch_fox, w_out, norm_ffn, w_up, w_down, norm_final):
    B = x_prompt.shape[0]
    meta = jnp.broadcast_to(meta_tokens.astype(x_prompt.dtype)[None], (B, N_META, D_MODEL))
    h_p = jnp.concatenate([meta, x_prompt], axis=1)
    h_s = x_sample
    st_p, st_s = [], []
    for l in range(DEPTH):
        w = (norm_mix[l], w_in[l], w_gla_gate[l], b_gla_gate[l], g_gla_norm[l], b_fox_forget[l],
             w_branch_gla[l], w_branch_fox[l], w_out[l], norm_ffn[l], w_up[l], w_down[l])
        h_p, sp = _layer_prompt(h_p, *w)
        h_s, ss = _layer_sample(h_s, cache_fox_k[l], cache_fox_v[l], cache_fox_logf[l], state_gla[l], *w)
        st_p.append(sp)
        st_s.append(ss)
    y_prompt = _rmsnorm(h_p[:, N_META:], norm_final)
    y_sample = _rmsnorm(h_s, norm_final)
    new_fox_k_prompt = jnp.stack([s[0] for s in st_p])
    new_fox_v_prompt = jnp.stack([s[1] for s in st_p])
    new_fox_logf_prompt = jnp.stack([s[2] for s in st_p])
    new_gla_state_prompt = jnp.stack([s[3] for s in st_p])
    new_fox_k_sample = jnp.stack([s[0] for s in st_s])
    new_fox_v_sample = jnp.stack([s[1] for s in st_s])
    new_fox_logf_sample = jnp.stack([s[2] for s in st_s])
    new_gla_state_sample = jnp.stack([s[3] for s in st_s])
    return (y_prompt, y_sample, new_fox_k_prompt, new_fox_v_prompt, new_fox_logf_prompt, new_gla_state_prompt,
            new_fox_k_sample, new_fox_v_sample, new_fox_logf_sample, new_gla_state_sample)
```

```python
import numpy as np
from contextlib import ExitStack
import concourse.bass as bass
import concourse.mybir as mybir
from concourse.bass_utils import run_bass_kernel_spmd

F32 = mybir.dt.float32
BF16 = mybir.dt.bfloat16
AF = mybir.ActivationFunctionType
ALU = mybir.AluOpType

D = 1024
NCH = 8
D_IN = 5144
D_FF = 4096
N_META = 16
EPS = 1e-6
C_GQ, C_GK, C_GV, C_GR, C_GLR = 0, 256, 512, 1024, 1536
C_FQ, C_FK, C_FV, C_FF = 1552, 2064, 2576, 3088
C_GA, C_GB = 3096, 4120
SLOT_ELEMS = 8448
NSLOT = 3
FBLK = 512
MASKW = 912
MASK0 = 400
DEBUG_STAGE = 99
DEBUG_SKIP_SAMPLE = False
DEBUG_CTX = 255
DEBUG_SUB = 99


class Buf:
    __slots__ = ("name", "w", "r")

    def __init__(self, name):
        self.name = name
        self.w = {}
        self.r = {}


class Eng:
    def __init__(self, name, kind):
        self.name = name
        self.kind = kind
        self.ops = []
        self.count = 0
        self.waited = {}
        self.ring = []
        self.ring_i = 0


class Prog:
    def __init__(self):
        self.eng = {
            "pe": Eng("pe", "pe"), "act": Eng("act", "cmp"), "dve": Eng("dve", "cmp"),
            "pool": Eng("pool", "dma"), "sp": Eng("sp", "dma"),
        }
        for i in range(12):
            self.eng["pool"].ring.append(["pool%d" % i, 0])
        for i in range(20):
            self.eng["sp"].ring.append(["sp%d" % i, 0])
        self.nops = 0
        self.stage = ''

    def sem_keys(self):
        ks = ["pe", "act", "dve"]
        ks += [r[0] for r in self.eng["pool"].ring]
        ks += [r[0] for r in self.eng["sp"].ring]
        return ks

    def emit(self, engname, fns, reads=(), writes=(), writes_add=()):
        e = self.eng[engname]
        if callable(fns):
            fns = [fns]
        need = {}

        def want(k, v, raw):
            if k == e.name:
                if e.kind == "pe":
                    return
            if v > need.get(k, 0):
                need[k] = v
        for b in reads:
            for k, v in b.w.items():
                want(k, v, True)
        for b in writes:
            for k, v in b.w.items():
                want(k, v, False)
            for k, v in b.r.items():
                want(k, v, False)
        for b in writes_add:
            for k, v in b.r.items():
                want(k, v, False)
        if e.kind == "dma":
            slot = e.ring[e.ring_i % len(e.ring)]
            e.ring_i += 1
            if slot[1] > 0:
                want(slot[0], slot[1], True)
            slot[1] += 16
            tok = (slot[0], slot[1])
        else:
            e.count += 1
            tok = (e.name, e.count)
        waits = []
        for k, v in need.items():
            if k == e.name:
                waits.append((k, v))
            elif e.waited.get(k, 0) < v:
                waits.append((k, v))
                e.waited[k] = v
        e.ops.append((waits, fns, tok, self.stage))
        self.nops += len(fns)
        for b in reads:
            if b.r.get(tok[0], 0) < tok[1]:
                b.r[tok[0]] = tok[1]
        for b in writes:
            b.w = {tok[0]: tok[1]}
            b.r = {}
        for b in writes_add:
            b.w[tok[0]] = tok[1]
        return tok

    def final_waits(self):
        e = self.eng["sp"]
        waits = []
        for k in ("pe", "act", "dve"):
            c = self.eng[k].count
            if c > 0:
                waits.append((k, c))
        for q in ("pool", "sp"):
            for s in self.eng[q].ring:
                if s[1] > 0:
                    waits.append((s[0], s[1]))
        e.ops.append((waits, [], None, 'end'))


class TileD:
    pass


def build_program(NP, NB, NTU, NS, PAST, TSMP):
    SEQ = NB * 128
    LP = N_META + SEQ
    NPT = PAST // 128
    NCX = max(1, NS)
    NKT = max(NB + 1, NCX * (NPT + 1))
    KTW = max(LP, NCX * (PAST + TSMP))
    nc = bass.Bass("TRN2", target_bir_lowering=False)
    P = Prog()
    es = ExitStack()

    def dram(name, shape, kind):
        return nc.dram_tensor(name, list(shape), F32, kind=kind).ap()
    I = {}
    for name, shape in [
        ("xp", (NP, SEQ, D)), ("xs", (NS, TSMP, D)), ("ck", (NS, 8, PAST, 64)), ("cv", (NS, 8, PAST, 64)),
        ("clf", (NS, 8, PAST)), ("sg", (NS, 4, 64, 128)), ("meta", (N_META, D)),
        ("norm_mix", (128, NCH)), ("w_in", (D, D_IN)), ("w_gate", (16, 256)), ("b_gate", (256,)),
        ("g_gla", (128, 1)), ("b_fox", (8,)), ("w_bg", (512, D)), ("w_bf", (512, D)), ("w_out", (D, D)),
        ("norm_ffn", (128, NCH)), ("w_up", (D, D_FF)), ("w_down", (D_FF, D)), ("norm_final", (D,)),
        ("c_ident", (128, 128)), ("c_trineg", (128, 128)), ("c_indneg", (128, 2)), ("c_triincl", (128, 128)),
        ("c_mask", (128, MASKW)),
    ]:
        I[name] = dram(name, shape, "ExternalInput")
    O = {}
    for name, shape in [
        ("y_p", (NP, SEQ, D)), ("y_s", (NS, TSMP, D)), ("k_p", (NP, 8, LP, 64)), ("v_p", (NP, 8, LP, 64)),
        ("lf_p", (NP, 8, LP)), ("g_p", (NP, 4, 64, 128)), ("k_s", (NS, 8, TSMP, 64)), ("v_s", (NS, 8, TSMP, 64)),
        ("lf_s", (NS, 8, TSMP)), ("g_s", (NS, 4, 64, 128)),
    ]:
        O[name] = dram(name, shape, "ExternalOutput")

    def sb(name, shape, dt=F32):
        return es.enter_context(nc.sbuf_tensor(name, list(shape), dt))

    NSL = NTU + 1
    TSMAX = N_META + NTU * 128
    X = sb("X", [128, NSL, D])
    xnT = sb("xnT", [128, NCH, TSMAX], BF16)
    gqT = sb("gqT", [128, 2, TSMAX], BF16)
    grT = sb("grT", [128, 4, TSMAX], BF16)
    glrT = sb("glrT", [16, TSMAX], BF16)
    yT = sb("yT", [128, 4, TSMAX], BF16)
    fqE = sb("fqE", [128, 4, TSMAX], BF16)
    fqO = sb("fqO", [128, 4, TSMAX], BF16)
    oT = sb("oT", [128, 4, TSMAX], BF16)
    mT = sb("mT", [128, NCH, TSMAX], BF16)
    aTs = [mT[:, 0:FBLK // 128, :], mT[:, FBLK // 128:2 * (FBLK // 128), :]]
    KT = sb("KT", [128, 4, KTW], BF16)
    V = sb("V", [128, NKT, 8, 65], BF16)
    cT = sb("cT", [128, NKT, 8])
    carry = sb("carry", [128, NCX, 8])
    cref = sb("cref", [128, 2, 8])
    biasG = sb("biasG", [128, 2, NKT, 8])
    S = sb("S", [128, NCX, 2, 128])
    Sb2 = [sb("Sb%d" % i, [128, 2, 128], BF16) for i in range(2)]
    slots = [sb("wslot%d" % i, [128, SLOT_ELEMS], BF16) for i in range(NSLOT)]
    stat = sb("stat", [128, NSL, 4])
    xb = [sb("xb%d" % i, [128, D], BF16) for i in range(2)]
    otok = sb("otok", [128, 4, 512], BF16)
    junk = otok[:, 0:2, :].rearrange("p a b -> p (a b)")
    R2 = 2
    sp_r = [sb("sp%d" % i, [128, 256]) for i in range(R2)]
    e_r = [sb("e%d" % i, [128, 256]) for i in range(R2)]
    kdec_r = [sb("kdec%d" % i, [128, 256], BF16) for i in range(R2)]
    v_r = [sb("v%d" % i, [128, 512], BF16) for i in range(R2)]
    eg_r = [sb("eg%d" % i, [128, 2, 2]) for i in range(R2)]
    sq_r = [sb("sq%d" % i, [128, 4, 128], BF16) for i in range(R2)]
    rs_r = [sb("rs%d" % i, [128, 4, 128]) for i in range(1)]
    kf_r = [sb("kf%d" % i, [128, 512]) for i in range(2)]
    vf_r = [sb("vf%d" % i, [128, 512]) for i in range(2)]
    kb_r = [sb("kb%d" % i, [128, 512], BF16) for i in range(R2)]
    lf_r = [sb("lf%d" % i, [128, 16]) for i in range(R2)]
    pT_r = [sb("pT%d" % i, [128, 512], BF16) for i in range(4)]
    rden_r = [sb("rden%d" % i, [128, 4]) for i in range(2)]
    sa_r = [sb("sa%d" % i, [128, 512]) for i in range(R2)]
    sb_r = [sb("sbg%d" % i, [128, 512]) for i in range(R2)]
    rl_r = [sb("rl%d" % i, [128, 512]) for i in range(R2)]
    ident = sb("ident", [128, 128], BF16)
    identf = sb("identf", [128, 128])
    trineg = sb("trineg", [128, 128])
    indneg = sb("indneg", [128, 2])
    triincl = sb("triincl", [128, 128])
    onesf = sb("onesf", [128, 128])
    onesb = sb("onesb", [128, 128], BF16)
    maskb = sb("maskb", [128, MASKW], BF16)
    gmix = sb("gmix", [128, NCH])
    gffn = sb("gffn", [128, NCH])
    gfin = sb("gfin", [128, D])
    ggla = sb("ggla", [128, 1])
    bgate = sb("bgate", [128, 256])
    bfox = sb("bfox", [128, 8])
    wgate = sb("wgate", [16, 256], BF16)
    clf_sb = [sb("clf_sb%d" % i, [8, 128]) for i in range(2)]
    kc_sb = otok[:, 0, :].rearrange("p (h d) -> p h d", h=8)
    lfT = sb("lfT", [8, TSMAX])

    psA = [es.enter_context(nc.psum_tensor("psA%d" % i, [128, 512], F32)) for i in range(4)]
    psB = [es.enter_context(nc.psum_tensor("psB%d" % i, [128, 512], F32)) for i in range(2)]
    psT = [es.enter_context(nc.psum_tensor("psT%d" % i, [128, 1024], BF16)) for i in range(2)]
    psA_b = [Buf("psA%d" % i) for i in range(4)]
    psB_b = [Buf("psB%d" % i) for i in range(2)]
    psT_b = [Buf("psT%d" % i) for i in range(2)]
    ctr = {"A": 0, "B": 0, "T": 0, "AB": 0, "S2": 0, "ACC": 0}

    def ps(pool):
        i = ctr[pool]
        ctr[pool] += 1
        if pool == "A":
            return psA[i % 4], psA_b[i % 4]
        if pool == "B":
            return psB[i % 2], psB_b[i % 2]
        if pool == "AB":
            k = i % 6
            return (psA[k], psA_b[k]) if k < 4 else (psB[k - 4], psB_b[k - 4])
        return psT[i % 2], psT_b[i % 2]

    B = {}

    def mk(name, n):
        B[name] = [Buf("%s%d" % (name, i)) for i in range(n)]
    for nm in ("X", "xnT", "gqT", "grT", "glrT", "yT", "fqT", "oT", "mT", "stat", "lfT"):
        mk(nm, NSL)
    mk("aT0", NSL)
    mk("aT1", NSL)
    for nm in ("KT", "V", "cT"):
        mk(nm, NKT)
    for nm, n in (("carry", NCX), ("cref", 2), ("biasG", 2), ("S", 2 * NCX), ("Sb0", 2), ("Sb1", 2), ("slot", NSLOT), ("xb", 2), ("junk", 1),
                  ("sp", R2), ("e", R2), ("kdec", R2), ("v", R2), ("eg", R2), ("sq", R2), ("rs", 1),
                  ("kf", 2), ("vf", 2), ("kb", R2), ("lf", R2), ("pT", 4), ("rden", 2), ("sa", R2), ("sbg", R2),
                  ("rl", R2), ("const", 1), ("clf", 2), ("kc", 1)):
        mk(nm, n)
    B["otok"] = B["junk"]
    B["kc"] = B["junk"]
    rc = {}

    def ring(name):
        i = rc.get(name, 0)
        rc[name] = i + 1
        return i % len(B[name])

    CONST = B["const"][0]

    def dma(q, out, in_, reads, writes, nonc=False, writes_add=()):
        if nonc:
            def f(e, out=out, in_=in_):
                with nc.allow_non_contiguous_dma(reason="small strided io"):
                    return e.dma_start(out=out, in_=in_)
        else:
            def f(e, out=out, in_=in_):
                return e.dma_start(out=out, in_=in_)
        P.emit(q, f, reads, writes, writes_add)

    cb = [Buf("c%d" % i) for i in range(20)]
    dma("pool", ident[:], I["c_ident"], [], [cb[0]])
    dma("sp", identf[:], I["c_ident"], [], [cb[1]])
    dma("sp", trineg[:], I["c_trineg"], [], [cb[2]])
    dma("sp", indneg[:], I["c_indneg"], [], [cb[3]])
    dma("sp", triincl[:], I["c_triincl"], [], [cb[4]])
    dma("pool", maskb[:], I["c_mask"], [], [cb[5]])
    dma("sp", gmix[:], I["norm_mix"], [], [cb[6]])
    dma("sp", gffn[:], I["norm_ffn"], [], [cb[7]])
    dma("sp", gfin[:], I["norm_final"].partition_broadcast(128), [], [cb[8]])
    dma("sp", ggla[:], I["g_gla"], [], [cb[9]])
    dma("sp", bgate[:], I["b_gate"].partition_broadcast(128), [], [cb[10]])
    dma("sp", bfox[:], I["b_fox"].partition_broadcast(128), [], [cb[11]])
    dma("pool", wgate[:], I["w_gate"], [], [cb[12]])
    P.emit("dve", lambda e: e.memset(onesf[:], 1.0), [], [cb[13]])
    P.emit("dve", lambda e: e.memset(onesb[:], 1.0), [], [cb[14]])
    for i_ in range(R2):
        P.emit("dve", lambda e, i_=i_: e.memset(sq_r[i_][:, :, :], 0.0), [], [B["sq"][i_]])
    P.emit("dve", lambda e: e.memset(V[:, :, :, :], 1.0), [], B["V"])
    P.emit("dve", lambda e: e.memset(fqE[:, :, :], 0.0), [], B["fqT"])
    P.emit("dve", lambda e: e.memset(fqO[:, :, :], 0.0), [], B["fqT"])
    CB = cb[:15]

    wl = {"i": 0, "li": 0, "first": True}
    NLOADS = 9 + D_FF // FBLK
    scr = nc.dram_tensor("wscr", [NLOADS, 128, SLOT_ELEMS], BF16, kind="Internal").ap()
    scr_b = [Buf("scr%d" % i) for i in range(NLOADS)]

    def load_weights(parts):
        si = wl["i"] % NSLOT
        wl["i"] += 1
        li = wl["li"]
        wl["li"] += 1
        slot, sbuf = slots[si], B["slot"][si]
        views = []
        off = 0
        for (ap, kc, cols) in parts:
            view = slot[:, off:off + kc * cols].rearrange("p (k n) -> p k n", k=kc)
            off += kc * cols
            views.append(view)
        assert off <= SLOT_ELEMS
        if wl["first"]:
            first = True
            for (ap, kc, cols), view in zip(parts, views):
                dma("pool", view, ap.rearrange("(k p) n -> p k n", p=128), [], [sbuf] if first else [],
                    writes_add=[] if first else [sbuf])
                first = False
            dma("sp", scr[li][:, :off], slot[:, :off], [sbuf], [scr_b[li]])
        else:
            dma("pool", slot[:, :off], scr[li][:, :off], [scr_b[li]], [sbuf])
        return views, sbuf

    def mm(out, lhsT, rhs, start, stop):
        return lambda e: e.matmul(out, lhsT=lhsT, rhs=rhs, start=start, stop=stop)

    def act(out, in_, func, reads, writes, bias=None, scale=None, accum=None):
        kw = {}
        if bias is not None:
            kw["bias"] = bias
        if scale is not None:
            kw["scale"] = scale
        if accum is not None:
            kw["accum_out"] = accum
        P.emit("act", lambda e: e.activation(out=out, in_=in_, func=func, **kw), reads, writes)

    def stats_part(t):
        n, j = t.n, t.slot
        st = stat[:, j, :]
        act(junk[:n, :], X[:n, j, :], AF.Square, [B["X"][j]], [B["junk"][0], B["stat"][j]], accum=st[:n, 0:1])
        act(st[:n, 1:2], st[:n, 0:1], AF.Ln, [B["stat"][j]], [B["stat"][j]], bias=EPS, scale=1.0 / D)
        act(st[:n, 2:3], st[:n, 1:2], AF.Exp, [B["stat"][j]], [B["stat"][j]], scale=-0.5)

    def scale_part(t):
        n, j = t.n, t.slot
        st = stat[:, j, :]
        xi = ring("xb")
        P.emit("dve", lambda e, st=st, n=n, j=j, xi=xi: e.tensor_scalar(
            out=xb[xi][:n, :], in0=X[:n, j, :], scalar1=st[:n, 2:3], scalar2=None, op0=ALU.mult),
            [B["X"][j], B["stat"][j]], [B["xb"][xi]])
        return xi

    def transpose_part(t, xi, gain):
        n, j, off = t.n, t.slot, t.off
        pt, ptb = ps("T")
        ptv = pt[:, :].rearrange("p (c n) -> p c n", c=NCH)
        P.emit("pe", [(lambda e, c=c, n=n, xi=xi, ptv=ptv: e.transpose(
            out=ptv[:, c, :n], in_=xb[xi][:n, c * 128:(c + 1) * 128], identity=ident[:n, :n])) for c in range(NCH)],
            [B["xb"][xi]] + CB, [ptb])
        P.emit("dve", lambda e, n=n, off=off, ptv=ptv, gain=gain: e.tensor_tensor(
            out=xnT[:, :, off:off + n], in0=ptv[:, :, :n],
            in1=gain[:, :].unsqueeze(2).to_broadcast([128, NCH, n]), op=ALU.mult),
            [ptb] + CB, [B["xnT"][j]])

    def norm_transpose(tl, gain, gbuf):
        prev = None
        for t in tl:
            stats_part(t)
            if prev is not None:
                transpose_part(prev[0], prev[1], gain)
            prev = (t, scale_part(t))
        transpose_part(prev[0], prev[1], gain)

    def tb(u, name, c0, c1):
        return [B[name][t.slot] for t in u.tiles if t.off < c1 and t.off + t.n > c0]

    def fm_linear(u, W, wbuf, wcols, KC, srcT, srcname, M, evac, pool="AB"):
        for (c0, c1) in u.segs:
            pt, ptb = ps(pool)
            P.emit("pe", [mm(pt[:M, :c1 - c0], W[:, kc, wcols[0]:wcols[1]], srcT[:, kc, c0:c1], kc == 0, kc == KC - 1)
                          for kc in range(KC)], [wbuf] + tb(u, srcname, c0, c1), [ptb])
            evac(pt, ptb, c0, c1)

    def logsig_neg(z_ps, zb, bias_t, n, w, out_sp, out_b, e_t, e_b, rd):
        P.emit("dve", lambda e: e.tensor_tensor(out=e_t[:n, :w], in0=z_ps[:n, :w], in1=bias_t[:n, :w], op=ALU.add),
               [zb] + CB + rd, [e_b])
        act(e_t[:n, :w], e_t[:n, :w], AF.Exp, [e_b], [e_b], scale=-1.0)
        act(out_sp[:n, :w], e_t[:n, :w], AF.Ln, [e_b], [out_b], bias=1.0)

    class Ctx:
        pass
    ctx = Ctx()

    def logf_tile(lf_ap, lf_b, n, kt, last_of_group, cx=0):
        pt, ptb = ps("A")
        P.emit("pe", mm(pt[:n, 0:8], triincl[:n, :n], lf_ap, True, True), [lf_b] + CB, [ptb])
        P.emit("dve", lambda e: e.tensor_tensor(out=cT[:n, kt, :], in0=pt[:n, 0:8], in1=carry[:n, cx, :], op=ALU.add),
               [ptb, B["carry"][cx]], [B["cT"][kt]])
        pt2, ptb2 = ps("A")
        P.emit("pe", mm(pt2[:, 0:8], onesf[:n, :], lf_ap, True, True), [lf_b] + CB, [ptb2])
        P.emit("dve", lambda e: e.tensor_tensor(out=carry[:, cx, :], in0=pt2[:, 0:8], in1=carry[:, cx, :], op=ALU.add),
               [ptb2, B["carry"][cx]], [B["carry"][cx]])
        if last_of_group is not None:
            g = last_of_group
            P.emit("dve", lambda e: e.tensor_copy(out=cref[:, g, :], in_=carry[:, cx, :]), [B["carry"][cx]], [B["cref"][g]])

    def ctx_begin_prompt():
        P.emit("dve", lambda e: e.memset(cT[:, :, :], 0.0), [], B["cT"])
        P.emit("dve", lambda e: e.memset(S[:, 0, :, :], 0.0), [], [B["S"][0], B["S"][1]])
        P.emit("dve", lambda e: e.memset(carry[:, 0, :], 0.0), [], [B["carry"][0]])

    def ctx_begin_sample(s, cx):
        vb, kb = cx * (NPT + 1), cx * (PAST + TSMP)
        P.emit("dve", lambda e: e.memset(cT[:, vb:vb + NPT + 1, :], 0.0), [], B["cT"][vb:vb + NPT + 1])
        sview = I["sg"][s].rearrange("(hp two) k v -> (two k) hp v", two=2)
        dma("sp", S[:, cx, :, :], sview, [], [B["S"][2 * cx], B["S"][2 * cx + 1]])
        P.emit("dve", lambda e: e.memset(carry[:, cx, :], 0.0), [], [B["carry"][cx]])
        for t in range(NPT):
            dma("pool", V[:, vb + t, :, 0:64],
                I["cv"][s][:, t * 128:(t + 1) * 128, :].rearrange("h p d -> p h d"), [], [B["V"][vb + t]])
            dma("pool", kc_sb[:, :, :], I["ck"][s][:, t * 128:(t + 1) * 128, :].rearrange("h p d -> p h d"),
                [], [B["kc"][0]])
            pt, ptb = ps("T")
            ptv = pt[:, 0:512].rearrange("p (c n) -> p c n", c=4)
            kcv = kc_sb[:, :, :].rearrange("p h d -> p (h d)")
            P.emit("pe", [(lambda e, hp=hp, ptv=ptv, kcv=kcv: e.transpose(
                out=ptv[:, hp, :], in_=kcv[:, hp * 128:(hp + 1) * 128], identity=ident[:, :])) for hp in range(4)],
                [B["kc"][0]] + CB, [ptb])
            P.emit("act", lambda e, t=t, ptv=ptv: e.copy(out=KT[:, :, kb + t * 128:kb + (t + 1) * 128], in_=ptv[:, :, :]),
                   [ptb], [B["KT"][vb + t]])
            cfi = ring("clf")
            dma("sp", clf_sb[cfi][:, :], I["clf"][s][:, t * 128:(t + 1) * 128], [], [B["clf"][cfi]])
            pl, plb = ps("A")
            P.emit("pe", lambda e, cfi=cfi, pl=pl: e.transpose(out=pl[:, 0:8], in_=clf_sb[cfi][:8, :],
                                                              identity=identf[:8, :8]), [B["clf"][cfi]] + CB, [plb])
            li = ring("lf")
            P.emit("act", lambda e, li=li, pl=pl: e.copy(out=lf_r[li][:, 0:8], in_=pl[:, 0:8]), [plb], [B["lf"][li]])
            logf_tile(lf_r[li][:, 0:8], B["lf"][li], 128, vb + t, None, cx)

    units = []
    for s in range(NP):
        alltiles = []
        t = TileD()
        t.kind, t.s, t.n, t.pos0, t.kt, t.src, t.out = "p", s, N_META, 0, 0, ("meta",), None
        t.cx, t.kcol = 0, 0
        t.chunks = [(0, N_META)]
        alltiles.append(t)
        for j in range(NB):
            t = TileD()
            t.kind, t.s, t.n, t.pos0, t.kt = "p", s, 128, N_META + 128 * j, j + 1
            t.cx, t.kcol = 0, N_META + 128 * j
            t.src, t.out = ("xp", s, 128 * j), ("y_p", s, 128 * j)
            t.chunks = [(0, 64), (64, 64)]
            alltiles.append(t)
        k = 0
        first = True
        while k < len(alltiles):
            take = NTU + 1 if first else NTU
            u = TileD()
            u.tiles = alltiles[k:k + take]
            u.begin = [("p", s, 0)] if first else []
            u.end = []
            k += take
            first = False
            units.append(u)
        units[-1].end = [("p", s, 0)]
    if NS > 0:
        u = TileD()
        u.tiles, u.begin, u.end = [], [], []
        for s in range(NS):
            t = TileD()
            t.kind, t.s, t.n, t.pos0 = "s", s, TSMP, PAST
            t.cx = s
            t.kt = s * (NPT + 1) + NPT
            t.kcol = s * (PAST + TSMP) + PAST
            t.src, t.out = ("xs", s), ("y_s", s)
            t.chunks = [(0, TSMP)]
            u.tiles.append(t)
            u.begin.append(("s", s, s))
            u.end.append(("s", s, s))
        units.append(u)
    for u in units:
        off = 0
        for i, t in enumerate(u.tiles):
            t.slot, t.off = i, off
            off += t.n
        u.TS = off
        nseg = (off + 511) // 512
        u.segs = []
        if nseg == 1:
            u.segs = [(0, off)]
        else:
            cur0 = 0
            for t in u.tiles:
                if t.off + t.n - cur0 > 512:
                    u.segs.append((cur0, t.off))
                    cur0 = t.off
            u.segs.append((cur0, off))
        u.otiles = [t for t in u.tiles if t.out is not None]
        o0, o1 = u.otiles[0].off, u.otiles[-1].off + u.otiles[-1].n
        u.osegs = []
        cur0 = o0
        for t in u.otiles:
            if t.off + t.n - cur0 > 512:
                u.osegs.append((cur0, t.off))
                cur0 = t.off
        u.osegs.append((cur0, o1))
        u.groups = []
        cur = []
        for t in u.tiles:
            if cur and (sum(x.n for x in cur) + t.n > 512 or (cur[-1].kind, cur[-1].s) != (t.kind, t.s)
                        or cur[-1].src[0] == "meta"):
                u.groups.append(cur)
                cur = []
            cur.append(t)
        u.groups.append(cur)

    w_in = I["w_in"]

    def load_x(t, q="sp"):
        t.loaded = True
        if t.src[0] == "meta":
            dma(q, X[:t.n, t.slot, :], I["meta"], [], [B["X"][t.slot]])
        elif t.src[0] == "xp":
            dma(q, X[:t.n, t.slot, :], I["xp"][t.src[1], t.src[2]:t.src[2] + 128, :], [], [B["X"][t.slot]])
        else:
            dma(q, X[:t.n, t.slot, :], I["xs"][t.src[1]], [], [B["X"][t.slot]])

    def final_stats(t):
        n, j = t.n, t.slot
        st = stat[:, j, :]
        act(junk[:n, :], X[:n, j, :], AF.Square, [B["X"][j]], [B["junk"][0], B["stat"][j]], accum=st[:n, 0:1])
        act(st[:n, 1:2], st[:n, 0:1], AF.Ln, [B["stat"][j]], [B["stat"][j]], bias=EPS, scale=1.0 / D)
        act(st[:n, 2:3], st[:n, 1:2], AF.Exp, [B["stat"][j]], [B["stat"][j]], scale=-0.5)

    def final_tile_apply(t):
        n, j = t.n, t.slot
        st = stat[:, j, :]
        P.emit("dve", lambda e, st=st, n=n, j=j: e.scalar_tensor_tensor(
            out=X[:n, j, :], in0=X[:n, j, :], scalar=st[:n, 2:3], in1=gfin[:n, :], op0=ALU.mult, op1=ALU.mult),
            [B["X"][j], B["stat"][j]] + CB, [B["X"][j]])
        if t.out[0] == "y_p":
            dst = O["y_p"][t.out[1], t.out[2]:t.out[2] + 128, :]
        else:
            dst = O["y_s"][t.out[1]]
        dma("sp", dst, X[:n, j, :], [B["X"][j]], [])

    for ui, u in enumerate(units):
        tiles = u.tiles
        if wl["li"] > 0:
            wl["first"] = False
        wl["li"] = 0
        P.stage = 'ctx'
        if DEBUG_STAGE <= 0:
            continue
        if DEBUG_SKIP_SAMPLE and tiles[0].kind == "s":
            continue
        for (kind_, s_, cx_) in u.begin:
            if kind_ == "p":
                ctx_begin_prompt()
            else:
                ctx_begin_sample(s_, cx_)
        P.stage = 'S1'
        for t in tiles:
            if not getattr(t, "loaded", False):
                load_x(t)
        norm_transpose(u.tiles, gmix, None)

        if DEBUG_STAGE <= 1:
            continue
        P.stage = 'S2'
        (WG2,), wgb2 = load_weights([(w_in[:, 1024:1552], NCH, 528)])
        (WG,), wgb = load_weights([(w_in[:, 0:1024], NCH, 1024)])
        for hp in range(2):
            def ev(pt, ptb, c0, c1, hp=hp):
                act(gqT[:, hp, c0:c1], pt[:, :c1 - c0], AF.Copy, [ptb], tb(u, "gqT", c0, c1), scale=0.125)
            fm_linear(u, WG, wgb, (C_GQ + hp * 128, C_GQ + (hp + 1) * 128), NCH, xnT, "xnT", 128, ev)
        gr_jobs = []
        for h in range(4):
            def ev(pt, ptb, c0, c1, h=h):
                act(grT[:, h, c0:c1], pt[:, :c1 - c0], AF.Silu, [ptb], tb(u, "grT", c0, c1))
            gr_jobs.append(lambda h=h, ev=ev: fm_linear(u, WG2, wgb2, (h * 128, (h + 1) * 128), NCH, xnT, "xnT", 128, ev, pool="B"))

        def ev(pt, ptb, c0, c1):
            P.emit("dve", lambda e: e.tensor_copy(out=glrT[:16, c0:c1], in_=pt[:16, :c1 - c0]), [ptb], tb(u, "glrT", c0, c1))
        fm_linear(u, WG2, wgb2, (512, 528), NCH, xnT, "xnT", 16, ev)

        if DEBUG_STAGE <= 2:
            continue
        P.stage = 'S3'
        def s3_tile(t_):
          for t in [t_]:
            n, j, off = t.n, t.slot, t.off
            pk, pkb = ps("A")
            P.emit("pe", [mm(pk[:n, 0:256], xnT[:, kc, off:off + n], WG[:, kc, C_GK:C_GK + 256], kc == 0, kc == NCH - 1)
                          for kc in range(NCH)], [wgb, B["xnT"][j]], [pkb])
            pv, pvb = ps("A")
            P.emit("pe", [mm(pv[:n, :], xnT[:, kc, off:off + n], WG[:, kc, C_GV:C_GV + 512], kc == 0, kc == NCH - 1)
                          for kc in range(NCH)], [wgb, B["xnT"][j]], [pvb])
            pz, pzb = ps("A")
            P.emit("pe", mm(pz[:n, 0:256], glrT[:16, off:off + n], wgate[:16, :], True, True),
                   [B["glrT"][j]] + CB, [pzb])
            while gr_jobs:
                gr_jobs.pop(0)()
            if DEBUG_SUB <= 1:
                continue
            ri = ring("sp")
            logsig_neg(pz, pzb, bgate, n, 256, sp_r[ri], B["sp"][ri], e_r[ri], B["e"][ri], [])
            vi = ring("v")
            P.emit("act", lambda e, vi=vi, pv=pv, n=n: e.copy(out=v_r[vi][:n, :], in_=pv[:n, :]), [pvb], [B["v"][vi]])
            if DEBUG_SUB <= 2:
                continue
            pD, pDb = ps("A")
            P.emit("pe", mm(pD[:n, 0:256], trineg[:n, :n], sp_r[ri][:n, :], True, True), [B["sp"][ri]] + CB, [pDb])
            act(e_r[ri][:n, :], pD[:n, 0:256], AF.Exp, [pDb], [B["e"][ri]])
            ki = ring("kdec")
            P.emit("dve", lambda e, ki=ki, pk=pk, ri=ri, n=n: e.tensor_tensor(
                out=kdec_r[ki][:n, :], in0=pk[:n, 0:256], in1=e_r[ri][:n, :], op=ALU.mult),
                [pkb, B["e"][ri]], [B["kdec"][ki]])
            if DEBUG_SUB <= 3:
                continue
            nch = len(t.chunks)
            gi = ring("eg")
            pg, pgb = ps("A")
            P.emit("pe", [mm(pg[:, hp * 2:hp * 2 + nch], sp_r[ri][:n, hp * 128:(hp + 1) * 128], indneg[:n, 0:nch], True, True)
                          for hp in range(2)], [B["sp"][ri]] + CB, [pgb])
            act(eg_r[gi][:, :, :nch], pg[:, 0:4].rearrange("p (a b) -> p a b", a=2)[:, :, :nch], AF.Exp, [pgb], [B["eg"][gi]])
            if DEBUG_SUB <= 4:
                continue
            po0, pob0 = ps("B")
            po1, pob1 = ps("B")
            povs = [po0[:, 0:256].rearrange("p (h n) -> p h n", h=2), po1[:, 0:256].rearrange("p (h n) -> p h n", h=2)]
            pobs = [pob0, pob1]
            pus = {}
            for ci, (r0, cl) in enumerate(t.chunks):
                for hp in range(2):
                    pu, pub = ps("A")
                    P.emit("pe", [mm(pu[half * 64:half * 64 + 64, 0:128],
                                     kdec_r[ki][r0:r0 + cl, hp * 128 + half * 64:hp * 128 + half * 64 + 64],
                                     v_r[vi][r0:r0 + cl, hp * 256 + half * 128:hp * 256 + half * 128 + 128], True, True)
                                  for half in range(2)], [B["kdec"][ki], B["v"][vi]], [pub])
                    pus[(ci, hp)] = (pu, pub)
            for ci, (r0, cl) in enumerate(t.chunks):
                for hp in range(2):
                    pu, pub = pus[(ci, hp)]
                    P.emit("dve", lambda e, hp=hp, gi=gi, ci=ci, pu=pu, cx=t.cx: e.scalar_tensor_tensor(
                        out=S[:, cx, hp, :], in0=S[:, cx, hp, :], scalar=eg_r[gi][:, hp, ci:ci + 1],
                        in1=pu[:, 0:128], op0=ALU.mult, op1=ALU.add),
                        [B["S"][2 * t.cx + hp], B["eg"][gi], pub], [B["S"][2 * t.cx + hp]])
                    sbi = ci % 2
                    P.emit("act", lambda e, hp=hp, sbi=sbi, cx=t.cx: e.copy(out=Sb2[sbi][:, hp, :], in_=S[:, cx, hp, :]),
                           [B["S"][2 * t.cx + hp]], [B["Sb%d" % sbi][hp]])
                for hp in range(2):
                    sbi = ci % 2
                    P.emit("pe", [mm(povs[half][:, hp, r0:r0 + cl], Sb2[sbi][half * 64:half * 64 + 64, hp, :],
                                     gqT[half * 64:half * 64 + 64, hp, off + r0:off + r0 + cl], True, True)
                                  for half in range(2)], [B["Sb%d" % sbi][hp], B["gqT"][j]], pobs)
            if DEBUG_SUB in (41, 42, 43):
                continue
            qi = ring("sq")
            for half in range(2):
                act(sq_r[qi][:, 2 * half:2 * half + 2, :n], povs[half][:, :, :n], AF.Square, [pobs[half]], [B["sq"][qi]])
            pss, pssb = ps("A")
            pssv = pss[:, :].rearrange("p (h n) -> p h n", h=4)
            P.emit("pe", mm(pss[:, :], onesb[:, :], sq_r[qi][:, :, :].rearrange("p h n -> p (h n)"), True, True),
                   [B["sq"][qi]] + CB, [pssb])
            si_ = ring("rs")
            act(rs_r[si_][:, :, :n], pssv[:, :, :n], AF.Ln, [pssb], [B["rs"][si_]], bias=EPS, scale=1.0 / 128)
            act(rs_r[si_][:, :, :n], rs_r[si_][:, :, :n], AF.Exp, [B["rs"][si_]], [B["rs"][si_]], scale=-0.5)
            rs4 = rs_r[si_][:, :, :n].rearrange("p (two hp) n -> p two hp n", two=2)
            gr4 = grT[:, :, off:off + n].rearrange("p (hp two) n -> p two hp n", two=2)
            P.emit("dve", lambda e, rs4=rs4, gr4=gr4: e.tensor_tensor(out=rs4, in0=rs4, in1=gr4, op=ALU.mult),
                   [B["rs"][si_], B["grT"][j]], [B["rs"][si_]])
            for half in range(2):
                yv = yT[:, :, off:off + n].rearrange("p (hp two) n -> p hp two n", two=2)[:, :, half, :]
                P.emit("dve", lambda e, si_=si_, n=n, half=half, yv=yv, pv_=povs[half]: e.scalar_tensor_tensor(
                    out=yv, in0=pv_[:, :, :n], scalar=ggla[:, 0:1], in1=rs_r[si_][:, 2 * half:2 * half + 2, :n],
                    op0=ALU.mult, op1=ALU.mult), [pobs[half], B["rs"][si_]] + CB, [B["yT"][j]])
        (WF,), wfb = load_weights([(w_in[:, C_FQ:C_FQ + 1024], NCH, 1024)])
        fq_jobs = []
        for hp in range(4):
            def ev(pt, ptb, c0, c1, hp=hp):
                act(fqE[0:64, hp, c0:c1], pt[0:64, :c1 - c0], AF.Copy, [ptb], tb(u, "fqT", c0, c1), scale=0.125)
                act(fqO[64:128, hp, c0:c1], pt[64:128, :c1 - c0], AF.Copy, [ptb], tb(u, "fqT", c0, c1), scale=0.125)
            fq_jobs.append(lambda hp=hp, ev=ev: fm_linear(u, WF, wfb, (hp * 128, (hp + 1) * 128), NCH, xnT, "xnT", 128, ev, pool="A"))
        for gi_, G in enumerate(u.groups):
            for t in G:
                t.glast = None
            G[-1].glast = gi_ % 2
        def s4_part1(t):
            n, j, off, kt = t.n, t.slot, t.off, t.kt
            kname, vname, lname = ("k_p", "v_p", "lf_p") if t.kind == "p" else ("k_s", "v_s", "lf_s")
            r0 = t.pos0 if t.kind == "p" else 0
            pk, pkb = ps("A")
            P.emit("pe", [mm(pk[:n, :], xnT[:, kc, off:off + n], WF[:, kc, 512:1024], kc == 0, kc == NCH - 1)
                          for kc in range(NCH)], [wfb, B["xnT"][j]], [pkb])
            pv, pvb = ps("A")
            P.emit("pe", [mm(pv[:n, :], xnT[:, kc, off:off + n], WF2[:, kc, 0:512], kc == 0, kc == NCH - 1)
                          for kc in range(NCH)], [wfb2, B["xnT"][j]], [pvb])
            pf, pfb = ps("A")
            P.emit("pe", [mm(pf[:n, 0:8], xnT[:, kc, off:off + n], WF2[:, kc, 512:520], kc == 0, kc == NCH - 1)
                          for kc in range(NCH)], [wfb2, B["xnT"][j]], [pfb])
            ki = ring("kf")
            act(kf_r[ki][:n, :], pk[:n, :], AF.Copy, [pkb], [B["kf"][ki]])
            bi = ring("kb")
            P.emit("dve", lambda e, bi=bi, ki=ki, n=n: e.tensor_copy(out=kb_r[bi][:n, :], in_=kf_r[ki][:n, :]), [B["kf"][ki]], [B["kb"][bi]])
            dma("sp", O[kname][t.s][:, r0:r0 + n, :].rearrange("h t d -> t h d"),
                kf_r[ki][:n, :].rearrange("p (h d) -> p h d", h=8), [B["kf"][ki]], [])
            vi = ring("vf")
            act(vf_r[vi][:n, :], pv[:n, :], AF.Copy, [pvb], [B["vf"][vi]])
            P.emit("dve", lambda e, vi=vi, n=n, kt=kt: e.tensor_copy(
                out=V[:n, kt, :, 0:64], in_=vf_r[vi][:n, :].rearrange("p (h d) -> p h d", h=8)), [B["vf"][vi]], [B["V"][kt]])
            dma("sp", O[vname][t.s][:, r0:r0 + n, :].rearrange("h t d -> t h d"),
                vf_r[vi][:n, :].rearrange("p (h d) -> p h d", h=8), [B["vf"][vi]], [])
            li = ring("lf")
            logsig_neg(pf, pfb, bfox, n, 8, lf_r[li][:, 8:16], B["lf"][li], lf_r[li][:, 0:8], B["lf"][li], [])
            P.emit("dve", lambda e, li=li, n=n: e.tensor_scalar(out=lf_r[li][:n, 0:8], in0=lf_r[li][:n, 8:16],
                                                               scalar1=-1.0, scalar2=None, op0=ALU.mult),
                   [B["lf"][li]], [B["lf"][li]])
            return bi, li

        def s4_part2(t, bi, li):
            n, kt = t.n, t.kt
            pt, ptb = ps("T")
            ptv = pt[:, 0:512].rearrange("p (c n) -> p c n", c=4)
            P.emit("pe", [(lambda e, hp=hp, ptv=ptv, bi=bi, n=n: e.transpose(
                out=ptv[:, hp, :n], in_=kb_r[bi][:n, hp * 128:(hp + 1) * 128], identity=ident[:n, :n])) for hp in range(4)],
                [B["kb"][bi]] + CB, [ptb])
            kcol = t.kcol
            P.emit("act", lambda e, ptv=ptv, n=n, kcol=kcol: e.copy(out=KT[:, :, kcol:kcol + n], in_=ptv[:, :, :n]),
                   [ptb], [B["KT"][kt]])
            logf_tile(lf_r[li][:n, 0:8], B["lf"][li], n, kt, t.glast, t.cx)
            pl, plb = ps("A")
            P.emit("pe", lambda e, pl=pl, li=li, n=n: e.transpose(out=pl[:8, :n], in_=lf_r[li][:n, 0:8], identity=identf[:n, :n]),
                   [B["lf"][li]] + CB, [plb])
            P.emit("dve", lambda e, pl=pl, n=n, off=t.off: e.tensor_copy(out=lfT[:8, off:off + n], in_=pl[:8, :n]),
                   [plb], [B["lfT"][t.slot]])

        prev = None
        WF2 = None
        for t in tiles:
            s3_tile(t)
            if WF2 is None:
                (WF2,), wfb2 = load_weights([(w_in[:, C_FV:C_FV + 520], NCH, 520)])
            if fq_jobs:
                fq_jobs.pop(0)()
            cur = (t,) + s4_part1(t)
            if prev is not None:
                s4_part2(*prev)
            prev = cur
        s4_part2(*prev)
        while fq_jobs:
            fq_jobs.pop(0)()
        for (kind, s_, cx_) in u.end:
            oview = O["g_p" if kind == "p" else "g_s"][s_].rearrange("(hp two) k v -> (two k) hp v", two=2)
            dma("sp", oview, S[:, cx_, :, :], [B["S"][2 * cx_], B["S"][2 * cx_ + 1]], [])
        runs = []
        for t in tiles:
            if runs and (runs[-1][0].kind, runs[-1][0].s) == (t.kind, t.s):
                runs[-1].append(t)
            else:
                runs.append([t])
        for rn in runs:
            t0_ = rn[0]
            lname = "lf_p" if t0_.kind == "p" else "lf_s"
            r0_ = t0_.pos0 if t0_.kind == "p" else 0
            w_ = sum(x.n for x in rn)
            dma("sp", O[lname][t0_.s][:, r0_:r0_ + w_], lfT[:8, t0_.off:t0_.off + w_], [B["lfT"][x.slot] for x in rn], [])

        if DEBUG_STAGE <= 4:
            continue
        P.stage = 'S5'
        def load_s6(hf):
            c0w = hf * 256
            return load_weights([
                (w_in[:, C_GA + c0w:C_GA + c0w + 256], NCH, 256), (w_in[:, C_GB + c0w:C_GB + c0w + 256], NCH, 256),
                (I["w_bg"][:, c0w:c0w + 256], 4, 256), (I["w_bf"][:, c0w:c0w + 256], 4, 256)])
        s6_pre = [load_s6(hf) for hf in range(3)]
        for gi_, G in enumerate(u.groups):
            g2 = gi_ % 2
            g0, g1 = G[0].off, G[-1].off + G[-1].n
            NQ = g1 - g0
            cx_ = G[0].cx
            vb = cx_ * (NPT + 1) if G[0].kind == "s" else 0
            kb = cx_ * (PAST + TSMP) if G[0].kind == "s" else 0
            lastkt = G[-1].kt
            nk = lastkt - vb + 1
            P.emit("dve", lambda e, g2=g2, nk=nk, vb=vb: e.tensor_tensor(
                out=biasG[:, g2, vb:vb + nk, :], in0=cref[:, g2, :].unsqueeze(1).to_broadcast([128, nk, 8]),
                in1=cT[:, vb:vb + nk, :], op=ALU.subtract),
                [B["cref"][g2]] + [B["cT"][vb + k] for k in range(nk)], [B["biasG"][g2]])
            if G[0].kind == "p":
                keyt = [(0, N_META, 0)] + [(k, 128, N_META + 128 * (k - 1)) for k in range(1, nk)]
            else:
                keyt = [(vb + k, 128, kb + 128 * k) for k in range(NPT)] + [(vb + NPT, TSMP, kb + PAST)]
            gpos0 = G[0].kcol
            its = [(h, idx, kt, n_k, kcol) for h in range(8) for idx, (kt, n_k, kcol) in enumerate(keyt)]
            qk = {}

            def emit_qk(i):
                h, idx, kt, n_k, kcol = its[i]
                hp, lo = h // 2, (h % 2) * 64
                bi_ = ctr["S2"] % 4
                ctr["S2"] += 1
                pss_, pssb_ = psA[bi_], psA_b[bi_]
                q0 = max(kcol - gpos0, 0)
                fq_ = fqE if lo == 0 else fqO
                P.emit("pe", mm(pss_[:n_k, q0:NQ], KT[:, hp, kcol:kcol + n_k], fq_[:, hp, g0 + q0:g1], True, True),
                       [B["KT"][kt]] + tb(u, "fqT", g0, g1), [pssb_])
                qk[i] = (pss_, pssb_)
            QAHEAD = 3
            for i_ in range(min(QAHEAD, len(its))):
                emit_qk(i_)
            tts = [(ti, t.off - g0, t.n) for ti, t in enumerate(G)]
            ntt = len(tts)
            OB = B["otok"][0]
            for i, (h, idx, kt, n_k, kcol) in enumerate(its):
                hp, lo = h // 2, (h % 2) * 64
                if i + QAHEAD < len(its):
                    emit_qk(i + QAHEAD)
                if idx == 0:
                    ai_ = ctr["ACC"] % 2
                    ctr["ACC"] += 1
                    accb, accbb = psB[ai_], psB_b[ai_]
                    accv = accb[:, 0:ntt * 65].rearrange("p (t c) -> p t c", c=65)
                    nmm = 0
                pss_, pssb_ = qk.pop(i)
                pi = ring("pT")
                d = kcol - gpos0
                q0 = max(d, 0)
                act(pT_r[pi][:n_k, q0:NQ], pss_[:n_k, q0:NQ], AF.Exp, [pssb_, B["biasG"][g2]], [B["pT"][pi]],
                    bias=biasG[:n_k, g2, kt, h:h + 1])
                if d >= 0:
                    m0 = MASK0 - d
                    P.emit("dve", lambda e, pi=pi, n_k=n_k, m0=m0, q0=q0: e.tensor_tensor(
                        out=pT_r[pi][:n_k, q0:q0 + n_k], in0=pT_r[pi][:n_k, q0:q0 + n_k], in1=maskb[:n_k, m0 + q0:m0 + q0 + n_k], op=ALU.mult),
                        [B["pT"][pi]] + CB, [B["pT"][pi]])
                last = idx == len(keyt) - 1
                fns = []
                live = sorted([(ti, tc0, tn) for (ti, tc0, tn) in tts if tc0 + tn > q0], key=lambda x: -x[2])
                if nmm == 0:
                    assert live[0][2] == max(x[2] for x in tts)
                for li_, (ti, tc0, tn) in enumerate(live):
                    fns.append(mm(accv[:tn, ti, :], pT_r[pi][:n_k, tc0:tc0 + tn], V[:n_k, kt, h, :], nmm == 0,
                                  last and li_ == len(live) - 1))
                    nmm += 1
                P.emit("pe", fns, [B["V"][kt], B["pT"][pi]], [accbb])
                if last:
                    di = ring("rden")
                    cls = []
                    for (ti, tc0, tn) in tts:
                        if cls and cls[-1][2] == tn:
                            cls[-1][1] = ti + 1
                        else:
                            cls.append([ti, ti + 1, tn])
                    for (ta, tb_, tn) in cls:
                        P.emit("dve", lambda e, di=di, ta=ta, tb_=tb_, tn=tn, accv=accv: e.reciprocal(
                            out=rden_r[di][:tn, ta:tb_], in_=accv[:tn, ta:tb_, 64]), [accbb], [B["rden"][di]])
                        P.emit("dve", lambda e, di=di, ta=ta, tb_=tb_, tn=tn, accv=accv, h=h: e.tensor_tensor(
                            out=otok[:tn, ta:tb_, h * 64:(h + 1) * 64], in0=accv[:tn, ta:tb_, 0:64],
                            in1=rden_r[di][:tn, ta:tb_].unsqueeze(2).to_broadcast([tn, tb_ - ta, 64]), op=ALU.mult),
                            [accbb, B["rden"][di]], [OB])
            for (ti, tc0, tn) in tts:
                pt, ptb = ps("T")
                ptv = pt[:, 0:512].rearrange("p (c n) -> p c n", c=4)
                P.emit("pe", [(lambda e, hp=hp, ptv=ptv, ti=ti, tn=tn: e.transpose(
                    out=ptv[:, hp, :tn], in_=otok[:tn, ti, hp * 128:(hp + 1) * 128], identity=ident[:tn, :tn])) for hp in range(4)],
                    [OB] + CB, [ptb])
                P.emit("dve", lambda e, ptv=ptv, tn=tn, c0=g0 + tc0: e.tensor_copy(out=oT[:, :, c0:c0 + tn], in_=ptv[:, :, :tn]),
                       [ptb], [B["oT"][G[ti].slot]])

        if DEBUG_STAGE <= 5:
            continue
        P.stage = 'S6'
        for hf in range(4):
            c0w = hf * 256
            (WA, WB, WBG, WBF), wmb = s6_pre[hf] if hf < len(s6_pre) else load_s6(hf)
            for dc in range(2):
                c = hf * 2 + dc
                ws = (dc * 128, (dc + 1) * 128)
                for (c0, c1) in u.osegs:
                    w = c1 - c0
                    pa, pab = ps("AB")
                    P.emit("pe", [mm(pa[:, :w], WA[:, kc, ws[0]:ws[1]], xnT[:, kc, c0:c1], kc == 0, kc == NCH - 1)
                                  for kc in range(NCH)], [wmb] + tb(u, "xnT", c0, c1), [pab])
                    pb_, pbb = ps("AB")
                    P.emit("pe", [mm(pb_[:, :w], WB[:, kc, ws[0]:ws[1]], xnT[:, kc, c0:c1], kc == 0, kc == NCH - 1)
                                  for kc in range(NCH)], [wmb] + tb(u, "xnT", c0, c1), [pbb])
                    pya, pyab = ps("AB")
                    P.emit("pe", [mm(pya[:, :w], WBG[:, kc, ws[0]:ws[1]], yT[:, kc, c0:c1], kc == 0, kc == 3)
                                  for kc in range(4)], [wmb] + tb(u, "yT", c0, c1), [pyab])
                    pyb, pybb = ps("AB")
                    P.emit("pe", [mm(pyb[:, :w], WBF[:, kc, ws[0]:ws[1]], oT[:, kc, c0:c1], kc == 0, kc == 3)
                                  for kc in range(4)], [wmb] + tb(u, "oT", c0, c1), [pybb])
                    ai, bi2 = ring("sa"), ring("sbg")
                    act(sa_r[ai][:, :w], pa[:, :w], AF.Sigmoid, [pab], [B["sa"][ai]])
                    act(sb_r[bi2][:, :w], pb_[:, :w], AF.Sigmoid, [pbb], [B["sbg"][bi2]])
                    P.emit("dve", lambda e, ai=ai, w=w, pya=pya: e.tensor_tensor(
                        out=sa_r[ai][:, :w], in0=sa_r[ai][:, :w], in1=pya[:, :w], op=ALU.mult), [B["sa"][ai], pyab], [B["sa"][ai]])
                    P.emit("dve", lambda e, bi2=bi2, w=w, pyb=pyb: e.tensor_tensor(
                        out=sb_r[bi2][:, :w], in0=sb_r[bi2][:, :w], in1=pyb[:, :w], op=ALU.mult), [B["sbg"][bi2], pybb], [B["sbg"][bi2]])
                    P.emit("dve", lambda e, ai=ai, bi2=bi2, w=w, c=c, c0=c0, c1=c1: e.tensor_tensor(
                        out=mT[:, c, c0:c1], in0=sa_r[ai][:, :w], in1=sb_r[bi2][:, :w], op=ALU.add),
                        [B["sa"][ai], B["sbg"][bi2]], tb(u, "mT", c0, c1) + tb(u, "aT0", c0, c1) + tb(u, "aT1", c0, c1))

        if DEBUG_STAGE <= 6:
            continue
        P.stage = 'S7'
        (WO,), wob = load_weights([(I["w_out"], NCH, D)])
        prev_t = None
        for t in u.otiles:
            n, j, off = t.n, t.slot, t.off
            for half in range(2):
                pt, ptb = ps("AB")
                P.emit("pe", [mm(pt[:n, :], mT[:, kc, off:off + n], WO[:, kc, half * 512:(half + 1) * 512], kc == 0, kc == NCH - 1)
                              for kc in range(NCH)], [wob, B["mT"][j]], [ptb])
                P.emit("dve", lambda e, n=n, j=j, half=half, pt=pt: e.tensor_tensor(
                    out=X[:n, j, half * 512:(half + 1) * 512], in0=X[:n, j, half * 512:(half + 1) * 512], in1=pt[:n, :], op=ALU.add),
                    [B["X"][j], ptb], [B["X"][j]])
            stats_part(t)
            if prev_t is not None:
                transpose_part(prev_t[0], prev_t[1], gffn)
            prev_t = (t, scale_part(t))
        transpose_part(prev_t[0], prev_t[1], gffn)

        if DEBUG_STAGE <= 8:
            continue
        P.stage = 'S9'
        NFC = FBLK // 128
        NFB = D_FF // FBLK
        ffw = {}

        def ffn_up(fb):
            (WU, WD), wub = load_weights([
                (I["w_up"][:, fb * FBLK:(fb + 1) * FBLK], NCH, FBLK),
                (I["w_down"][fb * FBLK:(fb + 1) * FBLK, :], NFC, D)])
            ffw[fb] = (WD, wub)
            aT, an = aTs[fb % 2], "aT%d" % (fb % 2)
            for fc in range(NFC):
                for (c0, c1) in u.osegs:
                    w = c1 - c0
                    pt, ptb = ps("AB")
                    P.emit("pe", [mm(pt[:, :w], WU[:, kc, fc * 128:(fc + 1) * 128], xnT[:, kc, c0:c1], kc == 0, kc == NCH - 1)
                                  for kc in range(NCH)], [wub] + tb(u, "xnT", c0, c1), [ptb])
                    ri = ring("rl")
                    act(rl_r[ri][:, :w], pt[:, :w], AF.Relu, [ptb], [B["rl"][ri]])
                    wr = tb(u, an, c0, c1) + (tb(u, "mT", c0, c1) if fb < 2 else [])
                    P.emit("dve", lambda e, ri=ri, w=w, fc=fc, c0=c0, c1=c1, aT=aT: e.tensor_tensor(
                        out=aT[:, fc, c0:c1], in0=rl_r[ri][:, :w], in1=rl_r[ri][:, :w], op=ALU.mult),
                        [B["rl"][ri]], wr)

        nxt = units[ui + 1] if ui + 1 < len(units) else None
        pend = [None]

        def final_apply(t):
            final_tile_apply(t)
            if nxt is not None:
                for t2 in nxt.tiles:
                    if t2.slot == t.slot:
                        load_x(t2, "pool")

        def ffn_down(fb):
            WD, wub = ffw.pop(fb)
            aT, an = aTs[fb % 2], "aT%d" % (fb % 2)
            if fb == NFB - 1 and nxt is not None:
                oslots = set(t.slot for t in u.otiles)
                for t2 in nxt.tiles:
                    if t2.slot not in oslots:
                        load_x(t2, "pool")
            for t in u.otiles:
                n, j, off = t.n, t.slot, t.off
                for half in range(2):
                    pt, ptb = ps("AB")
                    P.emit("pe", [mm(pt[:n, :], aT[:, fc, off:off + n], WD[:, fc, half * 512:(half + 1) * 512], fc == 0, fc == NFC - 1)
                                  for fc in range(NFC)], [wub, B[an][j]], [ptb])
                    P.emit("dve", lambda e, n=n, j=j, half=half, pt=pt: e.tensor_tensor(
                        out=X[:n, j, half * 512:(half + 1) * 512], in0=X[:n, j, half * 512:(half + 1) * 512], in1=pt[:n, :], op=ALU.add),
                        [B["X"][j], ptb], [B["X"][j]])
                if fb == NFB - 1:
                    final_stats(t)
                    if pend[0] is not None:
                        final_apply(pend[0])
                    pend[0] = t
            if fb == NFB - 1 and pend[0] is not None:
                final_apply(pend[0])
                pend[0] = None
        ffn_up(0)
        for fb in range(NFB):
            if fb + 1 < NFB:
                ffn_up(fb + 1)
            ffn_down(fb)


    P.final_waits()

    sems = {}
    for k in P.sem_keys():
        sems[k] = es.enter_context(nc.semaphore(k))
    with nc.Block() as block:
        def replay(engname):
            def run(e):
                for waits, fns, tok, _st in P.eng[engname].ops:
                    for (k, v) in waits:
                        e.wait_ge(sems[k], v)
                    ins = None
                    for f in fns:
                        ins = f(e)
                    if tok is not None and ins is not None:
                        ins.then_inc(sems[tok[0]], 16 if P.eng[engname].kind == "dma" else 1)
            return run
        block.tensor(replay("pe"))
        block.scalar(replay("act"))
        block.vector(replay("dve"))
        block.gpsimd(replay("pool"))
        block.sync(replay("sp"))
    P.sbuf_left = nc.sbuf_bytes_remaining
    es.close()
    return nc, P


def make_consts():
    c = {}
    c["c_ident"] = np.eye(128, dtype=np.float32)
    i = np.arange(128)
    same = (i[:, None] // 64) == (i[None, :] // 64)
    c["c_trineg"] = np.where(same & (i[:, None] > i[None, :]), -1.0 / 16, 0.0).astype(np.float32)
    ind = np.zeros((128, 2), np.float32)
    ind[:64, 0] = -1.0 / 16
    ind[64:, 1] = -1.0 / 16
    c["c_indneg"] = ind
    c["c_triincl"] = (i[:, None] <= i[None, :]).astype(np.float32)
    u = np.arange(MASKW) - MASK0
    c["c_mask"] = (u[None, :] - i[:, None] >= 0).astype(np.float32)
    return c


_CACHE = {}


def _get_program(key):
    if key not in _CACHE:
        _CACHE[key] = build_program(*key)
    return _CACHE[key]


def run_cores(inputs, n_cores, NP, NB, NTU, NS, PAST, TSMP):
    nc, P = _get_program((NP, NB, NTU, NS, PAST, TSMP))
    f = lambda a: np.ascontiguousarray(np.asarray(a, dtype=np.float32))
    consts = make_consts()
    shared = {
        "meta": f(inputs["meta_tokens"]),
        "norm_mix": f(np.asarray(inputs["norm_mix"])[0].reshape(NCH, 128).T),
        "w_in": f(inputs["w_in"][0]), "w_gate": f(inputs["w_gla_gate"][0]), "b_gate": f(inputs["b_gla_gate"][0]),
        "g_gla": f(np.asarray(inputs["g_gla_norm"])[0].reshape(128, 1)), "b_fox": f(inputs["b_fox_forget"][0]),
        "w_bg": f(inputs["w_branch_gla"][0]), "w_bf": f(inputs["w_branch_fox"][0]), "w_out": f(inputs["w_out"][0]),
        "norm_ffn": f(np.asarray(inputs["norm_ffn"])[0].reshape(NCH, 128).T),
        "w_up": f(inputs["w_up"][0]), "w_down": f(inputs["w_down"][0]), "norm_final": f(inputs["norm_final"]),
    }
    shared.update(consts)
    xp, xs = np.asarray(inputs["x_prompt"]), np.asarray(inputs["x_sample"])
    ck, cv = np.asarray(inputs["cache_fox_k"])[0], np.asarray(inputs["cache_fox_v"])[0]
    clf, sg = np.asarray(inputs["cache_fox_logf"])[0], np.asarray(inputs["state_gla"])[0]
    in_maps = []
    for i in range(n_cores):
        m = dict(shared)
        m["xp"] = f(xp[i * NP:(i + 1) * NP])
        m["xs"] = f(xs[i * NS:(i + 1) * NS])
        m["ck"] = f(ck[i * NS:(i + 1) * NS])
        m["cv"] = f(cv[i * NS:(i + 1) * NS])
        m["clf"] = f(clf[i * NS:(i + 1) * NS])
        m["sg"] = f(sg[i * NS:(i + 1) * NS])
        in_maps.append(m)
    res = run_bass_kernel_spmd(nc, in_maps, core_ids=list(range(n_cores)))
    R = res.results
    cat = lambda k: np.concatenate([np.asarray(r[k], dtype=np.float32) for r in R], axis=0)
    return (cat("y_p"), cat("y_s"), cat("k_p")[None], cat("v_p")[None], cat("lf_p")[None], cat("g_p")[None],
            cat("k_s")[None], cat("v_s")[None], cat("lf_s")[None], cat("g_s")[None])


def kernel(**inputs):
    return run_cores(inputs, 8, 2, 16, 4, 2, 1024, 32)
```

```python
import numpy as np
from contextlib import ExitStack
import concourse.bass as bass
import concourse.mybir as mybir
from concourse.bass_utils import run_bass_kernel_spmd

F32 = mybir.dt.float32
BF16 = mybir.dt.bfloat16
AF = mybir.ActivationFunctionType
ALU = mybir.AluOpType

D = 1024
NCH = 8
D_IN = 5144
D_FF = 4096
N_META = 16
EPS = 1e-6
C_GQ, C_GK, C_GV, C_GR, C_GLR = 0, 256, 512, 1024, 1536
C_FQ, C_FK, C_FV, C_FF = 1552, 2064, 2576, 3088
C_GA, C_GB = 3096, 4120
SLOT_ELEMS = 8448
NSLOT = 3
FBLK = 512
MASKW = 912
MASK0 = 400
DEBUG_STAGE = 99
DEBUG_SKIP_SAMPLE = False
DEBUG_CTX = 255
DEBUG_SUB = 99


class Buf:
    __slots__ = ("name", "w", "r")

    def __init__(self, name):
        self.name = name
        self.w = {}
        self.r = {}


class Eng:
    def __init__(self, name, kind):
        self.name = name
        self.kind = kind
        self.ops = []
        self.count = 0
        self.waited = {}
        self.ring = []
        self.ring_i = 0


class Prog:
    def __init__(self):
        self.eng = {
            "pe": Eng("pe", "pe"), "act": Eng("act", "cmp"), "dve": Eng("dve", "cmp"),
            "pool": Eng("pool", "dma"), "sp": Eng("sp", "dma"),
        }
        for i in range(12):
            self.eng["pool"].ring.append(["pool%d" % i, 0])
        for i in range(20):
            self.eng["sp"].ring.append(["sp%d" % i, 0])
        self.nops = 0
        self.stage = ''

    def sem_keys(self):
        ks = ["pe", "act", "dve"]
        ks += [r[0] for r in self.eng["pool"].ring]
        ks += [r[0] for r in self.eng["sp"].ring]
        return ks

    def emit(self, engname, fns, reads=(), writes=(), writes_add=()):
        e = self.eng[engname]
        if callable(fns):
            fns = [fns]
        need = {}

        def want(k, v, raw):
            if k == e.name:
                if e.kind == "pe":
                    return
            if v > need.get(k, 0):
                need[k] = v
        for b in reads:
            for k, v in b.w.items():
                want(k, v, True)
        for b in writes:
            for k, v in b.w.items():
                want(k, v, False)
            for k, v in b.r.items():
                want(k, v, False)
        for b in writes_add:
            for k, v in b.r.items():
                want(k, v, False)
        if e.kind == "dma":
            slot = e.ring[e.ring_i % len(e.ring)]
            e.ring_i += 1
            if slot[1] > 0:
                want(slot[0], slot[1], True)
            slot[1] += 16
            tok = (slot[0], slot[1])
        else:
            e.count += 1
            tok = (e.name, e.count)
        waits = []
        for k, v in need.items():
            if k == e.name:
                waits.append((k, v))
            elif e.waited.get(k, 0) < v:
                waits.append((k, v))
                e.waited[k] = v
        e.ops.append((waits, fns, tok, self.stage))
        self.nops += len(fns)
        for b in reads:
            if b.r.get(tok[0], 0) < tok[1]:
                b.r[tok[0]] = tok[1]
        for b in writes:
            b.w = {tok[0]: tok[1]}
            b.r = {}
        for b in writes_add:
            b.w[tok[0]] = tok[1]
        return tok

    def final_waits(self):
        e = self.eng["sp"]
        waits = []
        for k in ("pe", "act", "dve"):
            c = self.eng[k].count
            if c > 0:
                waits.append((k, c))
        for q in ("pool", "sp"):
            for s in self.eng[q].ring:
                if s[1] > 0:
                    waits.append((s[0], s[1]))
        e.ops.append((waits, [], None, 'end'))


class TileD:
    pass


def build_program(NP, NB, NTU, NS, PAST, TSMP):
    SEQ = NB * 128
    LP = N_META + SEQ
    NPT = PAST // 128
    NCX = max(1, NS)
    NKT = max(NB + 1, NCX * (NPT + 1))
    KTW = max(LP, NCX * (PAST + TSMP))
    nc = bass.Bass("TRN2", target_bir_lowering=False)
    P = Prog()
    es = ExitStack()

    def dram(name, shape, kind):
        return nc.dram_tensor(name, list(shape), F32, kind=kind).ap()
    I = {}
    for name, shape in [
        ("xp", (NP, SEQ, D)), ("xs", (NS, TSMP, D)), ("ck", (NS, 8, PAST, 64)), ("cv", (NS, 8, PAST, 64)),
        ("clf", (NS, 8, PAST)), ("sg", (NS, 4, 64, 128)), ("meta", (N_META, D)),
        ("norm_mix", (128, NCH)), ("w_in", (D, D_IN)), ("w_gate", (16, 256)), ("b_gate", (256,)),
        ("g_gla", (128, 1)), ("b_fox", (8,)), ("w_bg", (512, D)), ("w_bf", (512, D)), ("w_out", (D, D)),
        ("norm_ffn", (128, NCH)), ("w_up", (D, D_FF)), ("w_down", (D_FF, D)), ("norm_final", (D,)),
        ("c_ident", (128, 128)), ("c_trineg", (128, 128)), ("c_indneg", (128, 2)), ("c_triincl", (128, 128)),
        ("c_mask", (128, MASKW)),
    ]:
        I[name] = dram(name, shape, "ExternalInput")
    O = {}
    for name, shape in [
        ("y_p", (NP, SEQ, D)), ("y_s", (NS, TSMP, D)), ("k_p", (NP, 8, LP, 64)), ("v_p", (NP, 8, LP, 64)),
        ("lf_p", (NP, 8, LP)), ("g_p", (NP, 4, 64, 128)), ("k_s", (NS, 8, TSMP, 64)), ("v_s", (NS, 8, TSMP, 64)),
        ("lf_s", (NS, 8, TSMP)), ("g_s", (NS, 4, 64, 128)),
    ]:
        O[name] = dram(name, shape, "ExternalOutput")

    def sb(name, shape, dt=F32):
        return es.enter_context(nc.sbuf_tensor(name, list(shape), dt))

    NSL = NTU + 1
    TSMAX = N_META + NTU * 128
    X = sb("X", [128, NSL, D])
    xnT = sb("xnT", [128, NCH, TSMAX], BF16)
    gqT = sb("gqT", [128, 2, TSMAX], BF16)
    grT = sb("grT", [128, 4, TSMAX], BF16)
    glrT = sb("glrT", [16, TSMAX], BF16)
    yT = sb("yT", [128, 4, TSMAX], BF16)
    fqE = sb("fqE", [128, 4, TSMAX], BF16)
    fqO = sb("fqO", [128, 4, TSMAX], BF16)
    oT = sb("oT", [128, 4, TSMAX], BF16)
    mT = sb("mT", [128, NCH, TSMAX], BF16)
    aTs = [mT[:, 0:FBLK // 128, :], mT[:, FBLK // 128:2 * (FBLK // 128), :]]
    KT = sb("KT", [128, 4, KTW], BF16)
    V = sb("V", [128, NKT, 8, 65], BF16)
    cT = sb("cT", [128, NKT, 8])
    carry = sb("carry", [128, NCX, 8])
    cref = sb("cref", [128, 2, 8])
    biasG = sb("biasG", [128, 2, NKT, 8])
    S = sb("S", [128, NCX, 2, 128])
    Sb2 = [sb("Sb%d" % i, [128, 2, 128], BF16) for i in range(2)]
    slots = [sb("wslot%d" % i, [128, SLOT_ELEMS], BF16) for i in range(NSLOT)]
    stat = sb("stat", [128, NSL, 4])
    xb = [sb("xb%d" % i, [128, D], BF16) for i in range(2)]
    otok = sb("otok", [128, 4, 512], BF16)
    junk = otok[:, 0:2, :].rearrange("p a b -> p (a b)")
    R2 = 2
    sp_r = [sb("sp%d" % i, [128, 256]) for i in range(R2)]
    e_r = [sb("e%d" % i, [128, 256]) for i in range(R2)]
    kdec_r = [sb("kdec%d" % i, [128, 256], BF16) for i in range(R2)]
    v_r = [sb("v%d" % i, [128, 512], BF16) for i in range(R2)]
    eg_r = [sb("eg%d" % i, [128, 2, 2]) for i in range(R2)]
    sq_r = [sb("sq%d" % i, [128, 4, 128], BF16) for i in range(R2)]
    rs_r = [sb("rs%d" % i, [128, 4, 128]) for i in range(1)]
    kf_r = [sb("kf%d" % i, [128, 512]) for i in range(2)]
    vf_r = [sb("vf%d" % i, [128, 512]) for i in range(2)]
    kb_r = [sb("kb%d" % i, [128, 512], BF16) for i in range(R2)]
    lf_r = [sb("lf%d" % i, [128, 16]) for i in range(R2)]
    pT_r = [sb("pT%d" % i, [128, 512], BF16) for i in range(4)]
    rden_r = [sb("rden%d" % i, [128, 4]) for i in range(2)]
    sa_r = [sb("sa%d" % i, [128, 512]) for i in range(R2)]
    sb_r = [sb("sbg%d" % i, [128, 512]) for i in range(R2)]
    rl_r = [sb("rl%d" % i, [128, 512]) for i in range(R2)]
    ident = sb("ident", [128, 128], BF16)
    identf = sb("identf", [128, 128])
    trineg = sb("trineg", [128, 128])
    indneg = sb("indneg", [128, 2])
    triincl = sb("triincl", [128, 128])
    onesf = sb("onesf", [128, 128])
    onesb = sb("onesb", [128, 128], BF16)
    maskb = sb("maskb", [128, MASKW], BF16)
    gmix = sb("gmix", [128, NCH])
    gffn = sb("gffn", [128, NCH])
    gfin = sb("gfin", [128, D])
    ggla = sb("ggla", [128, 1])
    bgate = sb("bgate", [128, 256])
    bfox = sb("bfox", [128, 8])
    wgate = sb("wgate", [16, 256], BF16)
    clf_sb = [sb("clf_sb%d" % i, [8, 128]) for i in range(2)]
    kc_sb = otok[:, 0, :].rearrange("p (h d) -> p h d", h=8)
    lfT = sb("lfT", [8, TSMAX])

    psA = [es.enter_context(nc.psum_tensor("psA%d" % i, [128, 512], F32)) for i in range(4)]
    psB = [es.enter_context(nc.psum_tensor("psB%d" % i, [128, 512], F32)) for i in range(2)]
    psT = [es.enter_context(nc.psum_tensor("psT%d" % i, [128, 1024], BF16)) for i in range(2)]
    psA_b = [Buf("psA%d" % i) for i in range(4)]
    psB_b = [Buf("psB%d" % i) for i in range(2)]
    psT_b = [Buf("psT%d" % i) for i in range(2)]
    ctr = {"A": 0, "B": 0, "T": 0, "AB": 0, "S2": 0, "ACC": 0}

    def ps(pool):
        i = ctr[pool]
        ctr[pool] += 1
        if pool == "A":
            return psA[i % 4], psA_b[i % 4]
        if pool == "B":
            return psB[i % 2], psB_b[i % 2]
        if pool == "AB":
            k = i % 6
            return (psA[k], psA_b[k]) if k < 4 else (psB[k - 4], psB_b[k - 4])
        return psT[i % 2], psT_b[i % 2]

    B = {}

    def mk(name, n):
        B[name] = [Buf("%s%d" % (name, i)) for i in range(n)]
    for nm in ("X", "xnT", "gqT", "grT", "glrT", "yT", "fqT", "oT", "mT", "stat", "lfT"):
        mk(nm, NSL)
    mk("aT0", NSL)
    mk("aT1", NSL)
    for nm in ("KT", "V", "cT"):
        mk(nm, NKT)
    for nm, n in (("carry", NCX), ("cref", 2), ("biasG", 2), ("S", 2 * NCX), ("Sb0", 2), ("Sb1", 2), ("slot", NSLOT), ("xb", 2), ("junk", 1),
                  ("sp", R2), ("e", R2), ("kdec", R2), ("v", R2), ("eg", R2), ("sq", R2), ("rs", 1),
                  ("kf", 2), ("vf", 2), ("kb", R2), ("lf", R2), ("pT", 4), ("rden", 2), ("sa", R2), ("sbg", R2),
                  ("rl", R2), ("const", 1), ("clf", 2), ("kc", 1)):
        mk(nm, n)
    B["otok"] = B["junk"]
    B["kc"] = B["junk"]
    rc = {}

    def ring(name):
        i = rc.get(name, 0)
        rc[name] = i + 1
        return i % len(B[name])

    CONST = B["const"][0]

    def dma(q, out, in_, reads, writes, nonc=False, writes_add=()):
        if nonc:
            def f(e, out=out, in_=in_):
                with nc.allow_non_contiguous_dma(reason="small strided io"):
                    return e.dma_start(out=out, in_=in_)
        else:
            def f(e, out=out, in_=in_):
                return e.dma_start(out=out, in_=in_)
        P.emit(q, f, reads, writes, writes_add)

    cb = [Buf("c%d" % i) for i in range(20)]
    dma("pool", ident[:], I["c_ident"], [], [cb[0]])
    dma("sp", identf[:], I["c_ident"], [], [cb[1]])
    dma("sp", trineg[:], I["c_trineg"], [], [cb[2]])
    dma("sp", indneg[:], I["c_indneg"], [], [cb[3]])
    dma("sp", triincl[:], I["c_triincl"], [], [cb[4]])
    dma("pool", maskb[:], I["c_mask"], [], [cb[5]])
    dma("sp", gmix[:], I["norm_mix"], [], [cb[6]])
    dma("sp", gffn[:], I["norm_ffn"], [], [cb[7]])
    dma("sp", gfin[:], I["norm_final"].partition_broadcast(128), [], [cb[8]])
    dma("sp", ggla[:], I["g_gla"], [], [cb[9]])
    dma("sp", bgate[:], I["b_gate"].partition_broadcast(128), [], [cb[10]])
    dma("sp", bfox[:], I["b_fox"].partition_broadcast(128), [], [cb[11]])
    dma("pool", wgate[:], I["w_gate"], [], [cb[12]])
    P.emit("dve", lambda e: e.memset(onesf[:], 1.0), [], [cb[13]])
    P.emit("dve", lambda e: e.memset(onesb[:], 1.0), [], [cb[14]])
    for i_ in range(R2):
        P.emit("dve", lambda e, i_=i_: e.memset(sq_r[i_][:, :, :], 0.0), [], [B["sq"][i_]])
    P.emit("dve", lambda e: e.memset(V[:, :, :, :], 1.0), [], B["V"])
    P.emit("dve", lambda e: e.memset(fqE[:, :, :], 0.0), [], B["fqT"])
    P.emit("dve", lambda e: e.memset(fqO[:, :, :], 0.0), [], B["fqT"])
    CB = cb[:15]

    wl = {"i": 0, "li": 0, "first": True}
    NLOADS = 9 + D_FF // FBLK
    scr = nc.dram_tensor("wscr", [NLOADS, 128, SLOT_ELEMS], BF16, kind="Internal").ap()
    scr_b = [Buf("scr%d" % i) for i in range(NLOADS)]

    def load_weights(parts):
        si = wl["i"] % NSLOT
        wl["i"] += 1
        li = wl["li"]
        wl["li"] += 1
        slot, sbuf = slots[si], B["slot"][si]
        views = []
        off = 0
        for (ap, kc, cols) in parts:
            view = slot[:, off:off + kc * cols].rearrange("p (k n) -> p k n", k=kc)
            off += kc * cols
            views.append(view)
        assert off <= SLOT_ELEMS
        if wl["first"]:
            first = True
            for (ap, kc, cols), view in zip(parts, views):
                dma("pool", view, ap.rearrange("(k p) n -> p k n", p=128), [], [sbuf] if first else [],
                    writes_add=[] if first else [sbuf])
                first = False
            dma("sp", scr[li][:, :off], slot[:, :off], [sbuf], [scr_b[li]])
        else:
            dma("pool", slot[:, :off], scr[li][:, :off], [scr_b[li]], [sbuf])
        return views, sbuf

    def mm(out, lhsT, rhs, start, stop):
        return lambda e: e.matmul(out, lhsT=lhsT, rhs=rhs, start=start, stop=stop)

    def act(out, in_, func, reads, writes, bias=None, scale=None, accum=None):
        kw = {}
        if bias is not None:
            kw["bias"] = bias
        if scale is not None:
            kw["scale"] = scale
        if accum is not None:
            kw["accum_out"] = accum
        P.emit("act", lambda e: e.activation(out=out, in_=in_, func=func, **kw), reads, writes)

    def stats_part(t):
        n, j = t.n, t.slot
        st = stat[:, j, :]
        act(junk[:n, :], X[:n, j, :], AF.Square, [B["X"][j]], [B["junk"][0], B["stat"][j]], accum=st[:n, 0:1])
        act(st[:n, 1:2], st[:n, 0:1], AF.Ln, [B["stat"][j]], [B["stat"][j]], bias=EPS, scale=1.0 / D)
        act(st[:n, 2:3], st[:n, 1:2], AF.Exp, [B["stat"][j]], [B["stat"][j]], scale=-0.5)

    def scale_part(t):
        n, j = t.n, t.slot
        st = stat[:, j, :]
        xi = ring("xb")
        P.emit("dve", lambda e, st=st, n=n, j=j, xi=xi: e.tensor_scalar(
            out=xb[xi][:n, :], in0=X[:n, j, :], scalar1=st[:n, 2:3], scalar2=None, op0=ALU.mult),
            [B["X"][j], B["stat"][j]], [B["xb"][xi]])
        return xi

    def transpose_part(t, xi, gain):
        n, j, off = t.n, t.slot, t.off
        pt, ptb = ps("T")
        ptv = pt[:, :].rearrange("p (c n) -> p c n", c=NCH)
        P.emit("pe", [(lambda e, c=c, n=n, xi=xi, ptv=ptv: e.transpose(
            out=ptv[:, c, :n], in_=xb[xi][:n, c * 128:(c + 1) * 128], identity=ident[:n, :n])) for c in range(NCH)],
            [B["xb"][xi]] + CB, [ptb])
        P.emit("dve", lambda e, n=n, off=off, ptv=ptv, gain=gain: e.tensor_tensor(
            out=xnT[:, :, off:off + n], in0=ptv[:, :, :n],
            in1=gain[:, :].unsqueeze(2).to_broadcast([128, NCH, n]), op=ALU.mult),
            [ptb] + CB, [B["xnT"][j]])

    def norm_transpose(tl, gain, gbuf):
        prev = None
        for t in tl:
            stats_part(t)
            if prev is not None:
                transpose_part(prev[0], prev[1], gain)
            prev = (t, scale_part(t))
        transpose_part(prev[0], prev[1], gain)

    def tb(u, name, c0, c1):
        return [B[name][t.slot] for t in u.tiles if t.off < c1 and t.off + t.n > c0]

    def fm_linear(u, W, wbuf, wcols, KC, srcT, srcname, M, evac, pool="AB"):
        for (c0, c1) in u.segs:
            pt, ptb = ps(pool)
            P.emit("pe", [mm(pt[:M, :c1 - c0], W[:, kc, wcols[0]:wcols[1]], srcT[:, kc, c0:c1], kc == 0, kc == KC - 1)
                          for kc in range(KC)], [wbuf] + tb(u, srcname, c0, c1), [ptb])
            evac(pt, ptb, c0, c1)

    def logsig_neg(z_ps, zb, bias_t, n, w, out_sp, out_b, e_t, e_b, rd):
        P.emit("dve", lambda e: e.tensor_tensor(out=e_t[:n, :w], in0=z_ps[:n, :w], in1=bias_t[:n, :w], op=ALU.add),
               [zb] + CB + rd, [e_b])
        act(e_t[:n, :w], e_t[:n, :w], AF.Exp, [e_b], [e_b], scale=-1.0)
        act(out_sp[:n, :w], e_t[:n, :w], AF.Ln, [e_b], [out_b], bias=1.0)

    class Ctx:
        pass
    ctx = Ctx()

    def logf_tile(lf_ap, lf_b, n, kt, last_of_group, cx=0):
        pt, ptb = ps("A")
        P.emit("pe", mm(pt[:n, 0:8], triincl[:n, :n], lf_ap, True, True), [lf_b] + CB, [ptb])
        P.emit("dve", lambda e: e.tensor_tensor(out=cT[:n, kt, :], in0=pt[:n, 0:8], in1=carry[:n, cx, :], op=ALU.add),
               [ptb, B["carry"][cx]], [B["cT"][kt]])
        pt2, ptb2 = ps("A")
        P.emit("pe", mm(pt2[:, 0:8], onesf[:n, :], lf_ap, True, True), [lf_b] + CB, [ptb2])
        P.emit("dve", lambda e: e.tensor_tensor(out=carry[:, cx, :], in0=pt2[:, 0:8], in1=carry[:, cx, :], op=ALU.add),
               [ptb2, B["carry"][cx]], [B["carry"][cx]])
        if last_of_group is not None:
            g = last_of_group
            P.emit("dve", lambda e: e.tensor_copy(out=cref[:, g, :], in_=carry[:, cx, :]), [B["carry"][cx]], [B["cref"][g]])

    def ctx_begin_prompt():
        P.emit("dve", lambda e: e.memset(cT[:, :, :], 0.0), [], B["cT"])
        P.emit("dve", lambda e: e.memset(S[:, 0, :, :], 0.0), [], [B["S"][0], B["S"][1]])
        P.emit("dve", lambda e: e.memset(carry[:, 0, :], 0.0), [], [B["carry"][0]])

    def ctx_begin_sample(s, cx):
        vb, kb = cx * (NPT + 1), cx * (PAST + TSMP)
        P.emit("dve", lambda e: e.memset(cT[:, vb:vb + NPT + 1, :], 0.0), [], B["cT"][vb:vb + NPT + 1])
        sview = I["sg"][s].rearrange("(hp two) k v -> (two k) hp v", two=2)
        dma("sp", S[:, cx, :, :], sview, [], [B["S"][2 * cx], B["S"][2 * cx + 1]])
        P.emit("dve", lambda e: e.memset(carry[:, cx, :], 0.0), [], [B["carry"][cx]])
        for t in range(NPT):
            dma("pool", V[:, vb + t, :, 0:64],
                I["cv"][s][:, t * 128:(t + 1) * 128, :].rearrange("h p d -> p h d"), [], [B["V"][vb + t]])
            dma("pool", kc_sb[:, :, :], I["ck"][s][:, t * 128:(t + 1) * 128, :].rearrange("h p d -> p h d"),
                [], [B["kc"][0]])
            pt, ptb = ps("T")
            ptv = pt[:, 0:512].rearrange("p (c n) -> p c n", c=4)
            kcv = kc_sb[:, :, :].rearrange("p h d -> p (h d)")
            P.emit("pe", [(lambda e, hp=hp, ptv=ptv, kcv=kcv: e.transpose(
                out=ptv[:, hp, :], in_=kcv[:, hp * 128:(hp + 1) * 128], identity=ident[:, :])) for hp in range(4)],
                [B["kc"][0]] + CB, [ptb])
            P.emit("act", lambda e, t=t, ptv=ptv: e.copy(out=KT[:, :, kb + t * 128:kb + (t + 1) * 128], in_=ptv[:, :, :]),
                   [ptb], [B["KT"][vb + t]])
            cfi = ring("clf")
            dma("sp", clf_sb[cfi][:, :], I["clf"][s][:, t * 128:(t + 1) * 128], [], [B["clf"][cfi]])
            pl, plb = ps("A")
            P.emit("pe", lambda e, cfi=cfi, pl=pl: e.transpose(out=pl[:, 0:8], in_=clf_sb[cfi][:8, :],
                                                              identity=identf[:8, :8]), [B["clf"][cfi]] + CB, [plb])
            li = ring("lf")
            P.emit("act", lambda e, li=li, pl=pl: e.copy(out=lf_r[li][:, 0:8], in_=pl[:, 0:8]), [plb], [B["lf"][li]])
            logf_tile(lf_r[li][:, 0:8], B["lf"][li], 128, vb + t, None, cx)

    units = []
    for s in range(NP):
        alltiles = []
        t = TileD()
        t.kind, t.s, t.n, t.pos0, t.kt, t.src, t.out = "p", s, N_META, 0, 0, ("meta",), None
        t.cx, t.kcol = 0, 0
        t.chunks = [(0, N_META)]
        alltiles.append(t)
        for j in range(NB):
            t = TileD()
            t.kind, t.s, t.n, t.pos0, t.kt = "p", s, 128, N_META + 128 * j, j + 1
            t.cx, t.kcol = 0, N_META + 128 * j
            t.src, t.out = ("xp", s, 128 * j), ("y_p", s, 128 * j)
            t.chunks = [(0, 64), (64, 64)]
            alltiles.append(t)
        k = 0
        first = True
        while k < len(alltiles):
            take = NTU + 1 if first else NTU
            u = TileD()
            u.tiles = alltiles[k:k + take]
            u.begin = [("p", s, 0)] if first else []
            u.end = []
            k += take
            first = False
            units.append(u)
        units[-1].end = [("p", s, 0)]
    if NS > 0:
        u = TileD()
        u.tiles, u.begin, u.end = [], [], []
        for s in range(NS):
            t = TileD()
            t.kind, t.s, t.n, t.pos0 = "s", s, TSMP, PAST
            t.cx = s
            t.kt = s * (NPT + 1) + NPT
            t.kcol = s * (PAST + TSMP) + PAST
            t.src, t.out = ("xs", s), ("y_s", s)
            t.chunks = [(0, TSMP)]
            u.tiles.append(t)
            u.begin.append(("s", s, s))
            u.end.append(("s", s, s))
        units.append(u)
    for u in units:
        off = 0
        for i, t in enumerate(u.tiles):
            t.slot, t.off = i, off
            off += t.n
        u.TS = off
        nseg = (off + 511) // 512
        u.segs = []
        if nseg == 1:
            u.segs = [(0, off)]
        else:
            cur0 = 0
            for t in u.tiles:
                if t.off + t.n - cur0 > 512:
                    u.segs.append((cur0, t.off))
                    cur0 = t.off
            u.segs.append((cur0, off))
        u.otiles = [t for t in u.tiles if t.out is not None]
        o0, o1 = u.otiles[0].off, u.otiles[-1].off + u.otiles[-1].n
        u.osegs = []
        cur0 = o0
        for t in u.otiles:
            if t.off + t.n - cur0 > 512:
                u.osegs.append((cur0, t.off))
                cur0 = t.off
        u.osegs.append((cur0, o1))
        u.groups = []
        cur = []
        for t in u.tiles:
            if cur and (sum(x.n for x in cur) + t.n > 512 or (cur[-1].kind, cur[-1].s) != (t.kind, t.s)
                        or cur[-1].src[0] == "meta"):
                u.groups.append(cur)
                cur = []
            cur.append(t)
        u.groups.append(cur)

    w_in = I["w_in"]

    def load_x(t, q="sp"):
        t.loaded = True
        if t.src[0] == "meta":
            dma(q, X[:t.n, t.slot, :], I["meta"], [], [B["X"][t.slot]])
        elif t.src[0] == "xp":
            dma(q, X[:t.n, t.slot, :], I["xp"][t.src[1], t.src[2]:t.src[2] + 128, :], [], [B["X"][t.slot]])
        else:
            dma(q, X[:t.n, t.slot, :], I["xs"][t.src[1]], [], [B["X"][t.slot]])

    def final_stats(t):
        n, j = t.n, t.slot
        st = stat[:, j, :]
        act(junk[:n, :], X[:n, j, :], AF.Square, [B["X"][j]], [B["junk"][0], B["stat"][j]], accum=st[:n, 0:1])
        act(st[:n, 1:2], st[:n, 0:1], AF.Ln, [B["stat"][j]], [B["stat"][j]], bias=EPS, scale=1.0 / D)
        act(st[:n, 2:3], st[:n, 1:2], AF.Exp, [B["stat"][j]], [B["stat"][j]], scale=-0.5)

    def final_tile_apply(t):
        n, j = t.n, t.slot
        st = stat[:, j, :]
        P.emit("dve", lambda e, st=st, n=n, j=j: e.scalar_tensor_tensor(
            out=X[:n, j, :], in0=X[:n, j, :], scalar=st[:n, 2:3], in1=gfin[:n, :], op0=ALU.mult, op1=ALU.mult),
            [B["X"][j], B["stat"][j]] + CB, [B["X"][j]])
        if t.out[0] == "y_p":
            dst = O["y_p"][t.out[1], t.out[2]:t.out[2] + 128, :]
        else:
            dst = O["y_s"][t.out[1]]
        dma("sp", dst, X[:n, j, :], [B["X"][j]], [])

    for ui, u in enumerate(units):
        tiles = u.tiles
        if wl["li"] > 0:
            wl["first"] = False
        wl["li"] = 0
        P.stage = 'ctx'
        if DEBUG_STAGE <= 0:
            continue
        if DEBUG_SKIP_SAMPLE and tiles[0].kind == "s":
            continue
        for (kind_, s_, cx_) in u.begin:
            if kind_ == "p":
                ctx_begin_prompt()
            else:
                ctx_begin_sample(s_, cx_)
        P.stage = 'S1'
        for t in tiles:
            if not getattr(t, "loaded", False):
                load_x(t)
        norm_transpose(u.tiles, gmix, None)

        if DEBUG_STAGE <= 1:
            continue
        P.stage = 'S2'
        (WG2,), wgb2 = load_weights([(w_in[:, 1024:1552], NCH, 528)])
        (WG,), wgb = load_weights([(w_in[:, 0:1024], NCH, 1024)])
        for hp in range(2):
            def ev(pt, ptb, c0, c1, hp=hp):
                act(gqT[:, hp, c0:c1], pt[:, :c1 - c0], AF.Copy, [ptb], tb(u, "gqT", c0, c1), scale=0.125)
            fm_linear(u, WG, wgb, (C_GQ + hp * 128, C_GQ + (hp + 1) * 128), NCH, xnT, "xnT", 128, ev)
        gr_jobs = []
        for h in range(4):
            def ev(pt, ptb, c0, c1, h=h):
                act(grT[:, h, c0:c1], pt[:, :c1 - c0], AF.Silu, [ptb], tb(u, "grT", c0, c1))
            gr_jobs.append(lambda h=h, ev=ev: fm_linear(u, WG2, wgb2, (h * 128, (h + 1) * 128), NCH, xnT, "xnT", 128, ev, pool="A"))

        def ev(pt, ptb, c0, c1):
            P.emit("dve", lambda e: e.tensor_copy(out=glrT[:16, c0:c1], in_=pt[:16, :c1 - c0]), [ptb], tb(u, "glrT", c0, c1))
        fm_linear(u, WG2, wgb2, (512, 528), NCH, xnT, "xnT", 16, ev)

        if DEBUG_STAGE <= 2:
            continue
        P.stage = 'S3'
        def s3_tile(t_):
          for t in [t_]:
            n, j, off = t.n, t.slot, t.off
            pk, pkb = ps("A")
            P.emit("pe", [mm(pk[:n, 0:256], xnT[:, kc, off:off + n], WG[:, kc, C_GK:C_GK + 256], kc == 0, kc == NCH - 1)
                          for kc in range(NCH)], [wgb, B["xnT"][j]], [pkb])
            pv, pvb = ps("A")
            P.emit("pe", [mm(pv[:n, :], xnT[:, kc, off:off + n], WG[:, kc, C_GV:C_GV + 512], kc == 0, kc == NCH - 1)
                          for kc in range(NCH)], [wgb, B["xnT"][j]], [pvb])
            pz, pzb = ps("A")
            P.emit("pe", mm(pz[:n, 0:256], glrT[:16, off:off + n], wgate[:16, :], True, True),
                   [B["glrT"][j]] + CB, [pzb])
            if DEBUG_SUB <= 1:
                continue
            ri = ring("sp")
            logsig_neg(pz, pzb, bgate, n, 256, sp_r[ri], B["sp"][ri], e_r[ri], B["e"][ri], [])
            vi = ring("v")
            P.emit("act", lambda e, vi=vi, pv=pv, n=n: e.copy(out=v_r[vi][:n, :], in_=pv[:n, :]), [pvb], [B["v"][vi]])
            if DEBUG_SUB <= 2:
                continue
            pD, pDb = ps("A")
            P.emit("pe", mm(pD[:n, 0:256], trineg[:n, :n], sp_r[ri][:n, :], True, True), [B["sp"][ri]] + CB, [pDb])
            act(e_r[ri][:n, :], pD[:n, 0:256], AF.Exp, [pDb], [B["e"][ri]])
            ki = ring("kdec")
            P.emit("dve", lambda e, ki=ki, pk=pk, ri=ri, n=n: e.tensor_tensor(
                out=kdec_r[ki][:n, :], in0=pk[:n, 0:256], in1=e_r[ri][:n, :], op=ALU.mult),
                [pkb, B["e"][ri]], [B["kdec"][ki]])
            if DEBUG_SUB <= 3:
                continue
            nch = len(t.chunks)
            gi = ring("eg")
            pg, pgb = ps("A")
            P.emit("pe", [mm(pg[:, hp * 2:hp * 2 + nch], sp_r[ri][:n, hp * 128:(hp + 1) * 128], indneg[:n, 0:nch], True, True)
                          for hp in range(2)], [B["sp"][ri]] + CB, [pgb])
            act(eg_r[gi][:, :, :nch], pg[:, 0:4].rearrange("p (a b) -> p a b", a=2)[:, :, :nch], AF.Exp, [pgb], [B["eg"][gi]])
            if DEBUG_SUB <= 4:
                continue
            po0, pob0 = ps("B")
            po1, pob1 = ps("B")
            povs = [po0[:, 0:256].rearrange("p (h n) -> p h n", h=2), po1[:, 0:256].rearrange("p (h n) -> p h n", h=2)]
            pobs = [pob0, pob1]
            pus = {}
            for ci, (r0, cl) in enumerate(t.chunks):
                for hp in range(2):
                    pu, pub = ps("A")
                    P.emit("pe", [mm(pu[half * 64:half * 64 + 64, 0:128],
                                     kdec_r[ki][r0:r0 + cl, hp * 128 + half * 64:hp * 128 + half * 64 + 64],
                                     v_r[vi][r0:r0 + cl, hp * 256 + half * 128:hp * 256 + half * 128 + 128], True, True)
                                  for half in range(2)], [B["kdec"][ki], B["v"][vi]], [pub])
                    pus[(ci, hp)] = (pu, pub)
            for ci, (r0, cl) in enumerate(t.chunks):
                for hp in range(2):
                    pu, pub = pus[(ci, hp)]
                    P.emit("dve", lambda e, hp=hp, gi=gi, ci=ci, pu=pu, cx=t.cx: e.scalar_tensor_tensor(
                        out=S[:, cx, hp, :], in0=S[:, cx, hp, :], scalar=eg_r[gi][:, hp, ci:ci + 1],
                        in1=pu[:, 0:128], op0=ALU.mult, op1=ALU.add),
                        [B["S"][2 * t.cx + hp], B["eg"][gi], pub], [B["S"][2 * t.cx + hp]])
                    sbi = ci % 2
                    P.emit("act", lambda e, hp=hp, sbi=sbi, cx=t.cx: e.copy(out=Sb2[sbi][:, hp, :], in_=S[:, cx, hp, :]),
                           [B["S"][2 * t.cx + hp]], [B["Sb%d" % sbi][hp]])
                for hp in range(2):
                    sbi = ci % 2
                    P.emit("pe", [mm(povs[half][:, hp, r0:r0 + cl], Sb2[sbi][half * 64:half * 64 + 64, hp, :],
                                     gqT[half * 64:half * 64 + 64, hp, off + r0:off + r0 + cl], True, True)
                                  for half in range(2)], [B["Sb%d" % sbi][hp], B["gqT"][j]], pobs)
            yield
            qi = ring("sq")
            for half in range(2):
                act(sq_r[qi][:, 2 * half:2 * half + 2, :n], povs[half][:, :, :n], AF.Square, [pobs[half]], [B["sq"][qi]])
            pss, pssb = ps("A")
            pssv = pss[:, :].rearrange("p (h n) -> p h n", h=4)
            P.emit("pe", mm(pss[:, :], onesb[:, :], sq_r[qi][:, :, :].rearrange("p h n -> p (h n)"), True, True),
                   [B["sq"][qi]] + CB, [pssb])
            si_ = ring("rs")
            act(rs_r[si_][:, :, :n], pssv[:, :, :n], AF.Ln, [pssb], [B["rs"][si_]], bias=EPS, scale=1.0 / 128)
            act(rs_r[si_][:, :, :n], rs_r[si_][:, :, :n], AF.Exp, [B["rs"][si_]], [B["rs"][si_]], scale=-0.5)
            rs4 = rs_r[si_][:, :, :n].rearrange("p (two hp) n -> p two hp n", two=2)
            gr4 = grT[:, :, off:off + n].rearrange("p (hp two) n -> p two hp n", two=2)
            P.emit("dve", lambda e, rs4=rs4, gr4=gr4: e.tensor_tensor(out=rs4, in0=rs4, in1=gr4, op=ALU.mult),
                   [B["rs"][si_], B["grT"][j]], [B["rs"][si_]])
            for half in range(2):
                yv = yT[:, :, off:off + n].rearrange("p (hp two) n -> p hp two n", two=2)[:, :, half, :]
                P.emit("dve", lambda e, si_=si_, n=n, half=half, yv=yv, pv_=povs[half]: e.scalar_tensor_tensor(
                    out=yv, in0=pv_[:, :, :n], scalar=ggla[:, 0:1], in1=rs_r[si_][:, 2 * half:2 * half + 2, :n],
                    op0=ALU.mult, op1=ALU.mult), [pobs[half], B["rs"][si_]] + CB, [B["yT"][j]])
        (WF,), wfb = load_weights([(w_in[:, C_FQ:C_FQ + 1024], NCH, 1024)])
        fq_jobs = []
        for hp in range(4):
            def ev(pt, ptb, c0, c1, hp=hp):
                act(fqE[0:64, hp, c0:c1], pt[0:64, :c1 - c0], AF.Copy, [ptb], tb(u, "fqT", c0, c1), scale=0.125)
                act(fqO[64:128, hp, c0:c1], pt[64:128, :c1 - c0], AF.Copy, [ptb], tb(u, "fqT", c0, c1), scale=0.125)
            fq_jobs.append(lambda hp=hp, ev=ev: fm_linear(u, WF, wfb, (hp * 128, (hp + 1) * 128), NCH, xnT, "xnT", 128, ev, pool="A"))
        for gi_, G in enumerate(u.groups):
            for t in G:
                t.glast = None
            G[-1].glast = gi_ % 2
        def s4_part1(t):
            n, j, off, kt = t.n, t.slot, t.off, t.kt
            kname, vname, lname = ("k_p", "v_p", "lf_p") if t.kind == "p" else ("k_s", "v_s", "lf_s")
            r0 = t.pos0 if t.kind == "p" else 0
            pk, pkb = ps("A")
            P.emit("pe", [mm(pk[:n, :], xnT[:, kc, off:off + n], WF[:, kc, 512:1024], kc == 0, kc == NCH - 1)
                          for kc in range(NCH)], [wfb, B["xnT"][j]], [pkb])
            pv, pvb = ps("A")
            P.emit("pe", [mm(pv[:n, :], xnT[:, kc, off:off + n], WF2[:, kc, 0:512], kc == 0, kc == NCH - 1)
                          for kc in range(NCH)], [wfb2, B["xnT"][j]], [pvb])
            pf, pfb = ps("A")
            P.emit("pe", [mm(pf[:n, 0:8], xnT[:, kc, off:off + n], WF2[:, kc, 512:520], kc == 0, kc == NCH - 1)
                          for kc in range(NCH)], [wfb2, B["xnT"][j]], [pfb])
            ki = ring("kf")
            act(kf_r[ki][:n, :], pk[:n, :], AF.Copy, [pkb], [B["kf"][ki]])
            bi = ring("kb")
            P.emit("dve", lambda e, bi=bi, ki=ki, n=n: e.tensor_copy(out=kb_r[bi][:n, :], in_=kf_r[ki][:n, :]), [B["kf"][ki]], [B["kb"][bi]])
            dma("sp", O[kname][t.s][:, r0:r0 + n, :].rearrange("h t d -> t h d"),
                kf_r[ki][:n, :].rearrange("p (h d) -> p h d", h=8), [B["kf"][ki]], [])
            vi = ring("vf")
            act(vf_r[vi][:n, :], pv[:n, :], AF.Copy, [pvb], [B["vf"][vi]])
            P.emit("dve", lambda e, vi=vi, n=n, kt=kt: e.tensor_copy(
                out=V[:n, kt, :, 0:64], in_=vf_r[vi][:n, :].rearrange("p (h d) -> p h d", h=8)), [B["vf"][vi]], [B["V"][kt]])
            dma("sp", O[vname][t.s][:, r0:r0 + n, :].rearrange("h t d -> t h d"),
                vf_r[vi][:n, :].rearrange("p (h d) -> p h d", h=8), [B["vf"][vi]], [])
            li = ring("lf")
            logsig_neg(pf, pfb, bfox, n, 8, lf_r[li][:, 8:16], B["lf"][li], lf_r[li][:, 0:8], B["lf"][li], [])
            P.emit("dve", lambda e, li=li, n=n: e.tensor_scalar(out=lf_r[li][:n, 0:8], in0=lf_r[li][:n, 8:16],
                                                               scalar1=-1.0, scalar2=None, op0=ALU.mult),
                   [B["lf"][li]], [B["lf"][li]])
            return bi, li

        def s4_part2(t, bi, li):
            n, kt = t.n, t.kt
            pt, ptb = ps("T")
            ptv = pt[:, 0:512].rearrange("p (c n) -> p c n", c=4)
            P.emit("pe", [(lambda e, hp=hp, ptv=ptv, bi=bi, n=n: e.transpose(
                out=ptv[:, hp, :n], in_=kb_r[bi][:n, hp * 128:(hp + 1) * 128], identity=ident[:n, :n])) for hp in range(4)],
                [B["kb"][bi]] + CB, [ptb])
            kcol = t.kcol
            P.emit("act", lambda e, ptv=ptv, n=n, kcol=kcol: e.copy(out=KT[:, :, kcol:kcol + n], in_=ptv[:, :, :n]),
                   [ptb], [B["KT"][kt]])
            logf_tile(lf_r[li][:n, 0:8], B["lf"][li], n, kt, t.glast, t.cx)
            pl, plb = ps("A")
            P.emit("pe", lambda e, pl=pl, li=li, n=n: e.transpose(out=pl[:8, :n], in_=lf_r[li][:n, 0:8], identity=identf[:n, :n]),
                   [B["lf"][li]] + CB, [plb])
            P.emit("dve", lambda e, pl=pl, n=n, off=t.off: e.tensor_copy(out=lfT[:8, off:off + n], in_=pl[:8, :n]),
                   [plb], [B["lfT"][t.slot]])

        prev = None
        WF2 = None
        for t in tiles:
            g3 = s3_tile(t)
            next(g3)
            while gr_jobs:
                gr_jobs.pop(0)()
            for _ in g3:
                pass
            if WF2 is None:
                (WF2,), wfb2 = load_weights([(w_in[:, C_FV:C_FV + 520], NCH, 520)])
            if fq_jobs:
                fq_jobs.pop(0)()
            cur = (t,) + s4_part1(t)
            if prev is not None:
                s4_part2(*prev)
            prev = cur
        s4_part2(*prev)
        while fq_jobs:
            fq_jobs.pop(0)()
        for (kind, s_, cx_) in u.end:
            oview = O["g_p" if kind == "p" else "g_s"][s_].rearrange("(hp two) k v -> (two k) hp v", two=2)
            dma("sp", oview, S[:, cx_, :, :], [B["S"][2 * cx_], B["S"][2 * cx_ + 1]], [])
        runs = []
        for t in tiles:
            if runs and (runs[-1][0].kind, runs[-1][0].s) == (t.kind, t.s):
                runs[-1].append(t)
            else:
                runs.append([t])
        for rn in runs:
            t0_ = rn[0]
            lname = "lf_p" if t0_.kind == "p" else "lf_s"
            r0_ = t0_.pos0 if t0_.kind == "p" else 0
            w_ = sum(x.n for x in rn)
            dma("sp", O[lname][t0_.s][:, r0_:r0_ + w_], lfT[:8, t0_.off:t0_.off + w_], [B["lfT"][x.slot] for x in rn], [])

        if DEBUG_STAGE <= 4:
            continue
        P.stage = 'S5'
        for gi_, G in enumerate(u.groups):
            g2 = gi_ % 2
            g0, g1 = G[0].off, G[-1].off + G[-1].n
            NQ = g1 - g0
            cx_ = G[0].cx
            vb = cx_ * (NPT + 1) if G[0].kind == "s" else 0
            kb = cx_ * (PAST + TSMP) if G[0].kind == "s" else 0
            lastkt = G[-1].kt
            nk = lastkt - vb + 1
            P.emit("dve", lambda e, g2=g2, nk=nk, vb=vb: e.tensor_tensor(
                out=biasG[:, g2, vb:vb + nk, :], in0=cref[:, g2, :].unsqueeze(1).to_broadcast([128, nk, 8]),
                in1=cT[:, vb:vb + nk, :], op=ALU.subtract),
                [B["cref"][g2]] + [B["cT"][vb + k] for k in range(nk)], [B["biasG"][g2]])
            if G[0].kind == "p":
                keyt = [(0, N_META, 0)] + [(k, 128, N_META + 128 * (k - 1)) for k in range(1, nk)]
            else:
                keyt = [(vb + k, 128, kb + 128 * k) for k in range(NPT)] + [(vb + NPT, TSMP, kb + PAST)]
            gpos0 = G[0].kcol
            its = [(h, idx, kt, n_k, kcol) for h in range(8) for idx, (kt, n_k, kcol) in enumerate(keyt)]
            qk = {}

            def emit_qk(i):
                h, idx, kt, n_k, kcol = its[i]
                hp, lo = h // 2, (h % 2) * 64
                bi_ = ctr["S2"] % 4
                ctr["S2"] += 1
                pss_, pssb_ = psA[bi_], psA_b[bi_]
                q0 = max(kcol - gpos0, 0)
                fq_ = fqE if lo == 0 else fqO
                P.emit("pe", mm(pss_[:n_k, q0:NQ], KT[:, hp, kcol:kcol + n_k], fq_[:, hp, g0 + q0:g1], True, True),
                       [B["KT"][kt]] + tb(u, "fqT", g0, g1), [pssb_])
                qk[i] = (pss_, pssb_)
            QAHEAD = 3
            for i_ in range(min(QAHEAD, len(its))):
                emit_qk(i_)
            tts = [(ti, t.off - g0, t.n) for ti, t in enumerate(G)]
            ntt = len(tts)
            OB = B["otok"][0]
            for i, (h, idx, kt, n_k, kcol) in enumerate(its):
                hp, lo = h // 2, (h % 2) * 64
                if i + QAHEAD < len(its):
                    emit_qk(i + QAHEAD)
                if idx == 0:
                    ai_ = ctr["ACC"] % 2
                    ctr["ACC"] += 1
                    accb, accbb = psB[ai_], psB_b[ai_]
                    accv = accb[:, 0:ntt * 65].rearrange("p (t c) -> p t c", c=65)
                    nmm = 0
                pss_, pssb_ = qk.pop(i)
                pi = ring("pT")
                d = kcol - gpos0
                q0 = max(d, 0)
                act(pT_r[pi][:n_k, q0:NQ], pss_[:n_k, q0:NQ], AF.Exp, [pssb_, B["biasG"][g2]], [B["pT"][pi]],
                    bias=biasG[:n_k, g2, kt, h:h + 1])
                if d >= 0:
                    m0 = MASK0 - d
                    P.emit("dve", lambda e, pi=pi, n_k=n_k, m0=m0, q0=q0: e.tensor_tensor(
                        out=pT_r[pi][:n_k, q0:q0 + n_k], in0=pT_r[pi][:n_k, q0:q0 + n_k], in1=maskb[:n_k, m0 + q0:m0 + q0 + n_k], op=ALU.mult),
                        [B["pT"][pi]] + CB, [B["pT"][pi]])
                last = idx == len(keyt) - 1
                fns = []
                live = sorted([(ti, tc0, tn) for (ti, tc0, tn) in tts if tc0 + tn > q0], key=lambda x: -x[2])
                if nmm == 0:
                    assert live[0][2] == max(x[2] for x in tts)
                for li_, (ti, tc0, tn) in enumerate(live):
                    fns.append(mm(accv[:tn, ti, :], pT_r[pi][:n_k, tc0:tc0 + tn], V[:n_k, kt, h, :], nmm == 0,
                                  last and li_ == len(live) - 1))
                    nmm += 1
                P.emit("pe", fns, [B["V"][kt], B["pT"][pi]], [accbb])
                if last:
                    di = ring("rden")
                    cls = []
                    for (ti, tc0, tn) in tts:
                        if cls and cls[-1][2] == tn:
                            cls[-1][1] = ti + 1
                        else:
                            cls.append([ti, ti + 1, tn])
                    for (ta, tb_, tn) in cls:
                        P.emit("dve", lambda e, di=di, ta=ta, tb_=tb_, tn=tn, accv=accv: e.reciprocal(
                            out=rden_r[di][:tn, ta:tb_], in_=accv[:tn, ta:tb_, 64]), [accbb], [B["rden"][di]])
                        P.emit("dve", lambda e, di=di, ta=ta, tb_=tb_, tn=tn, accv=accv, h=h: e.tensor_tensor(
                            out=otok[:tn, ta:tb_, h * 64:(h + 1) * 64], in0=accv[:tn, ta:tb_, 0:64],
                            in1=rden_r[di][:tn, ta:tb_].unsqueeze(2).to_broadcast([tn, tb_ - ta, 64]), op=ALU.mult),
                            [accbb, B["rden"][di]], [OB])
            for (ti, tc0, tn) in tts:
                pt, ptb = ps("T")
                ptv = pt[:, 0:512].rearrange("p (c n) -> p c n", c=4)
                P.emit("pe", [(lambda e, hp=hp, ptv=ptv, ti=ti, tn=tn: e.transpose(
                    out=ptv[:, hp, :tn], in_=otok[:tn, ti, hp * 128:(hp + 1) * 128], identity=ident[:tn, :tn])) for hp in range(4)],
                    [OB] + CB, [ptb])
                P.emit("dve", lambda e, ptv=ptv, tn=tn, c0=g0 + tc0: e.tensor_copy(out=oT[:, :, c0:c0 + tn], in_=ptv[:, :, :tn]),
                       [ptb], [B["oT"][G[ti].slot]])

        if DEBUG_STAGE <= 5:
            continue
        P.stage = 'S6'
        for hf in range(4):
            c0w = hf * 256
            (WA, WB, WBG, WBF), wmb = load_weights([
                (w_in[:, C_GA + c0w:C_GA + c0w + 256], NCH, 256), (w_in[:, C_GB + c0w:C_GB + c0w + 256], NCH, 256),
                (I["w_bg"][:, c0w:c0w + 256], 4, 256), (I["w_bf"][:, c0w:c0w + 256], 4, 256)])
            for dc in range(2):
                c = hf * 2 + dc
                ws = (dc * 128, (dc + 1) * 128)
                for (c0, c1) in u.osegs:
                    w = c1 - c0
                    pa, pab = ps("AB")
                    P.emit("pe", [mm(pa[:, :w], WA[:, kc, ws[0]:ws[1]], xnT[:, kc, c0:c1], kc == 0, kc == NCH - 1)
                                  for kc in range(NCH)], [wmb] + tb(u, "xnT", c0, c1), [pab])
                    pb_, pbb = ps("AB")
                    P.emit("pe", [mm(pb_[:, :w], WB[:, kc, ws[0]:ws[1]], xnT[:, kc, c0:c1], kc == 0, kc == NCH - 1)
                                  for kc in range(NCH)], [wmb] + tb(u, "xnT", c0, c1), [pbb])
                    pya, pyab = ps("AB")
                    P.emit("pe", [mm(pya[:, :w], WBG[:, kc, ws[0]:ws[1]], yT[:, kc, c0:c1], kc == 0, kc == 3)
                                  for kc in range(4)], [wmb] + tb(u, "yT", c0, c1), [pyab])
                    pyb, pybb = ps("AB")
                    P.emit("pe", [mm(pyb[:, :w], WBF[:, kc, ws[0]:ws[1]], oT[:, kc, c0:c1], kc == 0, kc == 3)
                                  for kc in range(4)], [wmb] + tb(u, "oT", c0, c1), [pybb])
                    ai, bi2 = ring("sa"), ring("sbg")
                    act(sa_r[ai][:, :w], pa[:, :w], AF.Sigmoid, [pab], [B["sa"][ai]])
                    act(sb_r[bi2][:, :w], pb_[:, :w], AF.Sigmoid, [pbb], [B["sbg"][bi2]])
                    P.emit("dve", lambda e, ai=ai, w=w, pya=pya: e.tensor_tensor(
                        out=sa_r[ai][:, :w], in0=sa_r[ai][:, :w], in1=pya[:, :w], op=ALU.mult), [B["sa"][ai], pyab], [B["sa"][ai]])
                    P.emit("dve", lambda e, bi2=bi2, w=w, pyb=pyb: e.tensor_tensor(
                        out=sb_r[bi2][:, :w], in0=sb_r[bi2][:, :w], in1=pyb[:, :w], op=ALU.mult), [B["sbg"][bi2], pybb], [B["sbg"][bi2]])
                    P.emit("dve", lambda e, ai=ai, bi2=bi2, w=w, c=c, c0=c0, c1=c1: e.tensor_tensor(
                        out=mT[:, c, c0:c1], in0=sa_r[ai][:, :w], in1=sb_r[bi2][:, :w], op=ALU.add),
                        [B["sa"][ai], B["sbg"][bi2]], tb(u, "mT", c0, c1) + tb(u, "aT0", c0, c1) + tb(u, "aT1", c0, c1))

        if DEBUG_STAGE <= 6:
            continue
        P.stage = 'S7'
        (WO,), wob = load_weights([(I["w_out"], NCH, D)])
        prev_t = None
        for t in u.otiles:
            n, j, off = t.n, t.slot, t.off
            for half in range(2):
                pt, ptb = ps("AB")
                P.emit("pe", [mm(pt[:n, :], mT[:, kc, off:off + n], WO[:, kc, half * 512:(half + 1) * 512], kc == 0, kc == NCH - 1)
                              for kc in range(NCH)], [wob, B["mT"][j]], [ptb])
                P.emit("dve", lambda e, n=n, j=j, half=half, pt=pt: e.tensor_tensor(
                    out=X[:n, j, half * 512:(half + 1) * 512], in0=X[:n, j, half * 512:(half + 1) * 512], in1=pt[:n, :], op=ALU.add),
                    [B["X"][j], ptb], [B["X"][j]])
            stats_part(t)
            if prev_t is not None:
                transpose_part(prev_t[0], prev_t[1], gffn)
            prev_t = (t, scale_part(t))
        transpose_part(prev_t[0], prev_t[1], gffn)

        if DEBUG_STAGE <= 8:
            continue
        P.stage = 'S9'
        NFC = FBLK // 128
        NFB = D_FF // FBLK
        ffw = {}

        def ffn_up(fb):
            (WU, WD), wub = load_weights([
                (I["w_up"][:, fb * FBLK:(fb + 1) * FBLK], NCH, FBLK),
                (I["w_down"][fb * FBLK:(fb + 1) * FBLK, :], NFC, D)])
            ffw[fb] = (WD, wub)
            aT, an = aTs[fb % 2], "aT%d" % (fb % 2)
            for fc in range(NFC):
                for (c0, c1) in u.osegs:
                    w = c1 - c0
                    pt, ptb = ps("AB")
                    P.emit("pe", [mm(pt[:, :w], WU[:, kc, fc * 128:(fc + 1) * 128], xnT[:, kc, c0:c1], kc == 0, kc == NCH - 1)
                                  for kc in range(NCH)], [wub] + tb(u, "xnT", c0, c1), [ptb])
                    ri = ring("rl")
                    act(rl_r[ri][:, :w], pt[:, :w], AF.Relu, [ptb], [B["rl"][ri]])
                    wr = tb(u, an, c0, c1) + (tb(u, "mT", c0, c1) if fb < 2 else [])
                    P.emit("dve", lambda e, ri=ri, w=w, fc=fc, c0=c0, c1=c1, aT=aT: e.tensor_tensor(
                        out=aT[:, fc, c0:c1], in0=rl_r[ri][:, :w], in1=rl_r[ri][:, :w], op=ALU.mult),
                        [B["rl"][ri]], wr)

        nxt = units[ui + 1] if ui + 1 < len(units) else None
        pend = [None]

        def final_apply(t):
            final_tile_apply(t)
            if nxt is not None:
                for t2 in nxt.tiles:
                    if t2.slot == t.slot:
                        load_x(t2, "pool")

        def ffn_down(fb):
            WD, wub = ffw.pop(fb)
            aT, an = aTs[fb % 2], "aT%d" % (fb % 2)
            if fb == NFB - 1 and nxt is not None:
                oslots = set(t.slot for t in u.otiles)
                for t2 in nxt.tiles:
                    if t2.slot not in oslots:
                        load_x(t2, "pool")
            for t in u.otiles:
                n, j, off = t.n, t.slot, t.off
                for half in range(2):
                    pt, ptb = ps("AB")
                    P.emit("pe", [mm(pt[:n, :], aT[:, fc, off:off + n], WD[:, fc, half * 512:(half + 1) * 512], fc == 0, fc == NFC - 1)
                                  for fc in range(NFC)], [wub, B[an][j]], [ptb])
                    P.emit("dve", lambda e, n=n, j=j, half=half, pt=pt: e.tensor_tensor(
                        out=X[:n, j, half * 512:(half + 1) * 512], in0=X[:n, j, half * 512:(half + 1) * 512], in1=pt[:n, :], op=ALU.add),
                        [B["X"][j], ptb], [B["X"][j]])
                if fb == NFB - 1:
                    final_stats(t)
                    if pend[0] is not None:
                        final_apply(pend[0])
                    pend[0] = t
            if fb == NFB - 1 and pend[0] is not None:
                final_apply(pend[0])
                pend[0] = None
        ffn_up(0)
        for fb in range(NFB):
            if fb + 1 < NFB:
                ffn_up(fb + 1)
            ffn_down(fb)


    P.final_waits()

    sems = {}
    for k in P.sem_keys():
        sems[k] = es.enter_context(nc.semaphore(k))
    with nc.Block() as block:
        def replay(engname):
            def run(e):
                for waits, fns, tok, _st in P.eng[engname].ops:
                    for (k, v) in waits:
                        e.wait_ge(sems[k], v)
                    ins = None
                    for f in fns:
                        ins = f(e)
                    if tok is not None and ins is not None:
                        ins.then_inc(sems[tok[0]], 16 if P.eng[engname].kind == "dma" else 1)
            return run
        block.tensor(replay("pe"))
        block.scalar(replay("act"))
        block.vector(replay("dve"))
        block.gpsimd(replay("pool"))
        block.sync(replay("sp"))
    P.sbuf_left = nc.sbuf_bytes_remaining
    es.close()
    return nc, P


def make_consts():
    c = {}
    c["c_ident"] = np.eye(128, dtype=np.float32)
    i = np.arange(128)
    same = (i[:, None] // 64) == (i[None, :] // 64)
    c["c_trineg"] = np.where(same & (i[:, None] > i[None, :]), -1.0 / 16, 0.0).astype(np.float32)
    ind = np.zeros((128, 2), np.float32)
    ind[:64, 0] = -1.0 / 16
    ind[64:, 1] = -1.0 / 16
    c["c_indneg"] = ind
    c["c_triincl"] = (i[:, None] <= i[None, :]).astype(np.float32)
    u = np.arange(MASKW) - MASK0
    c["c_mask"] = (u[None, :] - i[:, None] >= 0).astype(np.float32)
    return c


_CACHE = {}


def _get_program(key):
    if key not in _CACHE:
        _CACHE[key] = build_program(*key)
    return _CACHE[key]


def run_cores(inputs, n_cores, NP, NB, NTU, NS, PAST, TSMP):
    nc, P = _get_program((NP, NB, NTU, NS, PAST, TSMP))
    f = lambda a: np.ascontiguousarray(np.asarray(a, dtype=np.float32))
    consts = make_consts()
    shared = {
        "meta": f(inputs["meta_tokens"]),
        "norm_mix": f(np.asarray(inputs["norm_mix"])[0].reshape(NCH, 128).T),
        "w_in": f(inputs["w_in"][0]), "w_gate": f(inputs["w_gla_gate"][0]), "b_gate": f(inputs["b_gla_gate"][0]),
        "g_gla": f(np.asarray(inputs["g_gla_norm"])[0].reshape(128, 1)), "b_fox": f(inputs["b_fox_forget"][0]),
        "w_bg": f(inputs["w_branch_gla"][0]), "w_bf": f(inputs["w_branch_fox"][0]), "w_out": f(inputs["w_out"][0]),
        "norm_ffn": f(np.asarray(inputs["norm_ffn"])[0].reshape(NCH, 128).T),
        "w_up": f(inputs["w_up"][0]), "w_down": f(inputs["w_down"][0]), "norm_final": f(inputs["norm_final"]),
    }
    shared.update(consts)
    xp, xs = np.asarray(inputs["x_prompt"]), np.asarray(inputs["x_sample"])
    ck, cv = np.asarray(inputs["cache_fox_k"])[0], np.asarray(inputs["cache_fox_v"])[0]
    clf, sg = np.asarray(inputs["cache_fox_logf"])[0], np.asarray(inputs["state_gla"])[0]
    in_maps = []
    for i in range(n_cores):
        m = dict(shared)
        m["xp"] = f(xp[i * NP:(i + 1) * NP])
        m["xs"] = f(xs[i * NS:(i + 1) * NS])
        m["ck"] = f(ck[i * NS:(i + 1) * NS])
        m["cv"] = f(cv[i * NS:(i + 1) * NS])
        m["clf"] = f(clf[i * NS:(i + 1) * NS])
        m["sg"] = f(sg[i * NS:(i + 1) * NS])
        in_maps.append(m)
    res = run_bass_kernel_spmd(nc, in_maps, core_ids=list(range(n_cores)))
    R = res.results
    cat = lambda k: np.concatenate([np.asarray(r[k], dtype=np.float32) for r in R], axis=0)
    return (cat("y_p"), cat("y_s"), cat("k_p")[None], cat("v_p")[None], cat("lf_p")[None], cat("g_p")[None],
            cat("k_s")[None], cat("v_s")[None], cat("lf_s")[None], cat("g_s")[None])


def kernel(**inputs):
    return run_cores(inputs, 8, 2, 16, 4, 2, 1024, 32)
```
